# Optimizing a Trainium2 kernel written in Bass

```python
import jax
import jax.numpy as jnp
from jax import lax
import numpy as np

D_MODEL = 1024
BATCH = 16
SEQ = 2048
DEPTH = 1

ROPE_THETA = 500000.0
NSA_HEADS = 8
NSA_KV_GROUPS = 2
NSA_GROUP = NSA_HEADS // NSA_KV_GROUPS
NSA_HEAD_DIM = 64
NSA_ROT_DIM = NSA_HEAD_DIM // 4
CMP_BLOCK = 32
CMP_STRIDE = 16
CMP_HIDDEN = 256
SEL_BLOCK = 64
SEL_TOPK = 8
WINDOW = 256
MLA_HEADS = 8
MLA_Q_RANK = 768
MLA_KV_RANK = 256
MLA_NOPE_DIM = 64
MLA_ROPE_DIM = 32
MLA_V_DIM = 64
MLA_QK_DIM = MLA_NOPE_DIM + MLA_ROPE_DIM
D_FF = 2816
CONV_WIDTH = 3
Q_BLOCK = 128
SEL_Q_BLOCK = 64
LN_EPS = 1e-5
RMS_EPS = 1e-6
DEEPNORM_ALPHA = (2 * DEPTH) ** 0.25
DEEPNORM_BETA = (8 * DEPTH) ** -0.25

NSA_Q_WIDTH = NSA_HEADS * NSA_HEAD_DIM
NSA_KV_WIDTH = NSA_KV_GROUPS * NSA_HEAD_DIM
IN_WIDTHS = (NSA_Q_WIDTH, NSA_KV_WIDTH, NSA_KV_WIDTH, NSA_KV_WIDTH, NSA_KV_WIDTH, NSA_KV_WIDTH, NSA_KV_WIDTH,
             3 * NSA_HEADS, MLA_Q_RANK, MLA_KV_RANK, MLA_ROPE_DIM, 2 * D_MODEL)
IN_TOTAL = sum(IN_WIDTHS)
IN_OFFSETS = tuple(int(v) for v in np.cumsum(IN_WIDTHS)[:-1])

kernel_name = 'hybrid_nsa_mla_convglu_deepnorm'


def layer_norm(x, g, b):
    xf = x.astype(jnp.float32)
    mu = jnp.mean(xf, axis=-1, keepdims=True)
    var = jnp.mean(jnp.square(xf - mu), axis=-1, keepdims=True)
    return ((xf - mu) * lax.rsqrt(var + LN_EPS) * g + b).astype(x.dtype)


def rms_norm(x, g):
    xf = x.astype(jnp.float32)
    return (xf * lax.rsqrt(jnp.mean(jnp.square(xf), axis=-1, keepdims=True) + RMS_EPS) * g).astype(x.dtype)


def rope_tables(rot_dim, seq):
    pos = jnp.arange(seq, dtype=jnp.float32)
    inv = ROPE_THETA ** (-jnp.arange(0, rot_dim, 2, dtype=jnp.float32) / rot_dim)
    ang = pos[:, None] * inv[None, :]
    return jnp.cos(ang), jnp.sin(ang)


def apply_partial_rope(x, cos, sin, rot_dim):
    half = rot_dim // 2
    c = cos[:, None, :].astype(x.dtype)
    s = sin[:, None, :].astype(x.dtype)
    x1, x2, xp = x[..., :half], x[..., half:rot_dim], x[..., rot_dim:]
    return jnp.concatenate([x1 * c - x2 * s, x2 * c + x1 * s, xp], axis=-1)


def masked_softmax(scores, mask):
    s = jnp.where(mask, scores.astype(jnp.float32), -jnp.inf)
    m = jnp.max(s, axis=-1, keepdims=True)
    m = jnp.where(jnp.isfinite(m), m, 0.0)
    e = jnp.where(mask, jnp.exp(s - m), 0.0)
    return e / jnp.maximum(jnp.sum(e, axis=-1, keepdims=True), 1e-30)


def compress_tokens(tok, pe, w1, b1, w2):
    b, s, g, d = tok.shape
    n_cmp = (s - CMP_BLOCK) // CMP_STRIDE + 1
    idx = jnp.arange(n_cmp)[:, None] * CMP_STRIDE + jnp.arange(CMP_BLOCK)[None, :]
    blk = tok[:, idx] + pe[None, None, :, None, :]
    blk = jnp.swapaxes(blk, 2, 3).reshape(b, n_cmp, g, CMP_BLOCK * d)
    return jax.nn.gelu(blk @ w1 + b1) @ w2


def nsa_attention(q, k_cmp, v_cmp, k_sel, v_sel, k_win, v_win, gates,
                  pe_k, pe_v, ck_w1, ck_b1, ck_w2, cv_w1, cv_b1, cv_w2):
    b, s = q.shape[:2]
    scale = NSA_HEAD_DIM ** -0.5
    pos = jnp.arange(s)

    kc = compress_tokens(k_cmp, pe_k, ck_w1, ck_b1, ck_w2)
    vc = compress_tokens(v_cmp, pe_v, cv_w1, cv_b1, cv_w2)
    n_cmp = kc.shape[1]
    cmp_start = jnp.arange(n_cmp) * CMP_STRIDE
    cmp_end = cmp_start + CMP_BLOCK - 1
    sc = jnp.einsum('bshgd,bnhd->bhgsn', q, kc) * scale
    p_cmp = masked_softmax(sc, cmp_end[None, :] <= pos[:, None])
    o_cmp = jnp.einsum('bhgsn,bnhd->bshgd', p_cmp.astype(vc.dtype), vc)

    n_sel = s // SEL_BLOCK
    k_top = min(SEL_TOPK, n_sel)
    sel_start = jnp.arange(n_sel) * SEL_BLOCK
    overlap = ((cmp_start[:, None] <= sel_start[None, :] + SEL_BLOCK - 1)
               & (cmp_end[:, None] >= sel_start[None, :])).astype(jnp.float32)
    imp = jnp.einsum('bhgsn,nj->bhsj', p_cmp, overlap)
    blk_id = jnp.arange(n_sel)[None, :]
    cur = (pos // SEL_BLOCK)[:, None]
    valid = sel_start[None, :] <= pos[:, None]
    forced = (blk_id == 0) | (blk_id == cur) | (blk_id == cur - 1)
    imp = jnp.where(forced & valid, jnp.inf, imp)
    imp = jnp.where(valid, imp, -jnp.inf)
    top_val, top_idx = lax.top_k(imp, k_top)
    top_ok = top_val > -jnp.inf
    k_blocks = jnp.transpose(k_sel.reshape(b, n_sel, SEL_BLOCK, NSA_KV_GROUPS, NSA_HEAD_DIM), (0, 3, 1, 2, 4))
    v_blocks = jnp.transpose(v_sel.reshape(b, n_sel, SEL_BLOCK, NSA_KV_GROUPS, NSA_HEAD_DIM), (0, 3, 1, 2, 4))
    bi = jnp.arange(b)[:, None, None, None]
    hi = jnp.arange(NSA_KV_GROUPS)[None, :, None, None]

    def sel_block(i):
        s0 = i * SEL_Q_BLOCK
        qb = lax.dynamic_slice_in_dim(q, s0, SEL_Q_BLOCK, axis=1)
        ib = lax.dynamic_slice_in_dim(top_idx, s0, SEL_Q_BLOCK, axis=2)
        okb = lax.dynamic_slice_in_dim(top_ok, s0, SEL_Q_BLOCK, axis=2)
        kg = k_blocks[bi, hi, ib]
        vg = v_blocks[bi, hi, ib]
        sc_s = jnp.einsum('bqhgd,bhqkld->bhgqkl', qb, kg) * scale
        kpos = ib[..., None] * SEL_BLOCK + jnp.arange(SEL_BLOCK)
        tq = s0 + jnp.arange(SEL_Q_BLOCK)
        mask = (kpos <= tq[None, None, :, None, None]) & okb[..., None]
        flat = k_top * SEL_BLOCK
        p = masked_softmax(sc_s.reshape(b, NSA_KV_GROUPS, NSA_GROUP, SEL_Q_BLOCK, flat),
                           mask.reshape(b, NSA_KV_GROUPS, 1, SEL_Q_BLOCK, flat))
        p = p.reshape(b, NSA_KV_GROUPS, NSA_GROUP, SEL_Q_BLOCK, k_top, SEL_BLOCK).astype(vg.dtype)
        return jnp.einsum('bhgqkl,bhqkld->bqhgd', p, vg)

    o_sel = lax.map(sel_block, jnp.arange(s // SEL_Q_BLOCK))
    o_sel = jnp.moveaxis(o_sel, 0, 1).reshape(q.shape)

    kw_pad = jnp.pad(k_win, ((0, 0), (WINDOW, 0), (0, 0), (0, 0)))
    vw_pad = jnp.pad(v_win, ((0, 0), (WINDOW, 0), (0, 0), (0, 0)))

    def win_block(i):
        s0 = i * Q_BLOCK
        qb = lax.dynamic_slice_in_dim(q, s0, Q_BLOCK, axis=1)
        kb = lax.dynamic_slice_in_dim(kw_pad, s0, WINDOW + Q_BLOCK, axis=1)
        vb = lax.dynamic_slice_in_dim(vw_pad, s0, WINDOW + Q_BLOCK, axis=1)
        sc_w = jnp.einsum('bqhgd,bkhd->bhgqk', qb, kb) * scale
        kpos = s0 - WINDOW + jnp.arange(WINDOW + Q_BLOCK)
        tq = s0 + jnp.arange(Q_BLOCK)
        mask = (kpos[None, :] >= 0) & (kpos[None, :] <= tq[:, None]) & (tq[:, None] - kpos[None, :] < WINDOW)
        p = masked_softmax(sc_w, mask).astype(vb.dtype)
        return jnp.einsum('bhgqk,bkhd->bqhgd', p, vb)

    o_win = lax.map(win_block, jnp.arange(s // Q_BLOCK))
    o_win = jnp.moveaxis(o_win, 0, 1).reshape(q.shape)

    o = gates[..., 0:1] * o_cmp + gates[..., 1:2] * o_sel + gates[..., 2:3] * o_win
    return o.reshape(b, s, NSA_Q_WIDTH)


def mla_attention(c_q, c_kv, k_rope_raw, q_norm, w_uq, kv_norm, w_uk, w_uv, cos, sin):
    b, s, _ = c_q.shape
    q = (rms_norm(c_q, q_norm) @ w_uq).reshape(b, s, MLA_HEADS, MLA_QK_DIM)
    q_nope = q[..., :MLA_NOPE_DIM]
    q_rope = apply_partial_rope(q[..., MLA_NOPE_DIM:], cos, sin, MLA_ROPE_DIM)
    c = rms_norm(c_kv, kv_norm)
    k_nope = (c @ w_uk).reshape(b, s, MLA_HEADS, MLA_NOPE_DIM)
    v = (c @ w_uv).reshape(b, s, MLA_HEADS, MLA_V_DIM)
    k_rope = apply_partial_rope(k_rope_raw[:, :, None, :], cos, sin, MLA_ROPE_DIM)[:, :, 0]
    scale = MLA_QK_DIM ** -0.5
    kpos = jnp.arange(s)

    def block(i):
        s0 = i * Q_BLOCK
        qn = lax.dynamic_slice_in_dim(q_nope, s0, Q_BLOCK, axis=1)
        qr = lax.dynamic_slice_in_dim(q_rope, s0, Q_BLOCK, axis=1)
        sc = (jnp.einsum('bqhd,bkhd->bhqk', qn, k_nope) + jnp.einsum('bqhd,bkd->bhqk', qr, k_rope)) * scale
        tq = s0 + jnp.arange(Q_BLOCK)
        p = masked_softmax(sc, kpos[None, :] <= tq[:, None]).astype(v.dtype)
        return jnp.einsum('bhqk,bkhd->bqhd', p, v)

    o = lax.map(block, jnp.arange(s // Q_BLOCK))
    return jnp.moveaxis(o, 0, 1).reshape(b, s, MLA_HEADS * MLA_V_DIM)


def conv_glu(x, w_gate, w_up, conv_w, conv_b, w_down):
    a = x @ w_gate
    a = lax.conv_general_dilated(a, conv_w[:, None, :], window_strides=(1,), padding=[(CONV_WIDTH - 1, 0)],
                                 dimension_numbers=('NWC', 'WIO', 'NWC'), feature_group_count=D_FF) + conv_b
    return (jax.nn.gelu(a) * (x @ w_up)) @ w_down


def setup_inputs(seed: int = 0) -> dict:
    key = jax.random.key(seed)
    ks = jax.random.split(key, 27)

    def nrm(k, shape, scale):
        return jax.random.normal(k, (DEPTH,) + shape, jnp.float32) * scale

    def gain(k, n):
        return 1.0 + 0.02 * jax.random.normal(k, (DEPTH, n), jnp.float32)

    dk = NSA_HEAD_DIM
    flat = CMP_BLOCK * dk
    return {
        'x': jax.random.normal(ks[0], (BATCH, SEQ, D_MODEL), jnp.float32),
        'w_in': nrm(ks[1], (D_MODEL, IN_TOTAL), D_MODEL ** -0.5),
        'cmp_pe_k': nrm(ks[2], (CMP_BLOCK, dk), 0.1),
        'cmp_pe_v': nrm(ks[3], (CMP_BLOCK, dk), 0.1),
        'cmp_k_w1': nrm(ks[4], (flat, CMP_HIDDEN), flat ** -0.5),
        'cmp_k_b1': nrm(ks[5], (CMP_HIDDEN,), 0.02),
        'cmp_k_w2': nrm(ks[6], (CMP_HIDDEN, dk), CMP_HIDDEN ** -0.5),
        'cmp_v_w1': nrm(ks[7], (flat, CMP_HIDDEN), flat ** -0.5),
        'cmp_v_b1': nrm(ks[8], (CMP_HIDDEN,), 0.02),
        'cmp_v_w2': nrm(ks[9], (CMP_HIDDEN, dk), CMP_HIDDEN ** -0.5),
        'nsa_w_o': nrm(ks[10], (NSA_Q_WIDTH, D_MODEL), NSA_Q_WIDTH ** -0.5),
        'mla_q_norm': gain(ks[11], MLA_Q_RANK),
        'mla_w_uq': nrm(ks[12], (MLA_Q_RANK, MLA_HEADS * MLA_QK_DIM), MLA_Q_RANK ** -0.5),
        'mla_kv_norm': gain(ks[13], MLA_KV_RANK),
        'mla_w_uk': nrm(ks[14], (MLA_KV_RANK, MLA_HEADS * MLA_NOPE_DIM), MLA_KV_RANK ** -0.5),
        'mla_w_uv': nrm(ks[15], (MLA_KV_RANK, MLA_HEADS * MLA_V_DIM), MLA_KV_RANK ** -0.5),
        'mla_w_o': nrm(ks[16], (MLA_HEADS * MLA_V_DIM, D_MODEL), (MLA_HEADS * MLA_V_DIM) ** -0.5),
        'w_out': nrm(ks[17], (D_MODEL, D_MODEL), DEEPNORM_BETA * D_MODEL ** -0.5),
        'ln1_g': gain(ks[18], D_MODEL),
        'ln1_b': nrm(ks[19], (D_MODEL,), 0.02),
        'ffn_w_gate': nrm(ks[20], (D_MODEL, D_FF), D_MODEL ** -0.5),
        'ffn_w_up': nrm(ks[21], (D_MODEL, D_FF), D_MODEL ** -0.5),
        'ffn_conv_w': nrm(ks[22], (CONV_WIDTH, D_FF), CONV_WIDTH ** -0.5),
        'ffn_conv_b': nrm(ks[23], (D_FF,), 0.02),
        'ffn_w_down': nrm(ks[24], (D_FF, D_MODEL), DEEPNORM_BETA * D_FF ** -0.5),
        'ln2_g': gain(ks[25], D_MODEL),
        'ln2_b': nrm(ks[26], (D_MODEL,), 0.02),
    }


def reference(x, w_in, cmp_pe_k, cmp_pe_v, cmp_k_w1, cmp_k_b1, cmp_k_w2, cmp_v_w1, cmp_v_b1, cmp_v_w2,
              nsa_w_o, mla_q_norm, mla_w_uq, mla_kv_norm, mla_w_uk, mla_w_uv, mla_w_o, w_out,
              ln1_g, ln1_b, ffn_w_gate, ffn_w_up, ffn_conv_w, ffn_conv_b, ffn_w_down, ln2_g, ln2_b):
    b, s, _ = x.shape
    cos_n, sin_n = rope_tables(NSA_ROT_DIM, s)
    cos_m, sin_m = rope_tables(MLA_ROPE_DIM, s)
    for l in range(DEPTH):
        h = x @ w_in[l]
        (nq, kc, vc, ksl, vsl, kw, vw, ng, cq, ckv, kr, mg) = jnp.split(h, IN_OFFSETS, axis=-1)

        kvs = (b, s, NSA_KV_GROUPS, NSA_HEAD_DIM)
        q = apply_partial_rope(nq.reshape(b, s, NSA_HEADS, NSA_HEAD_DIM), cos_n, sin_n, NSA_ROT_DIM)
        q = q.reshape(b, s, NSA_KV_GROUPS, NSA_GROUP, NSA_HEAD_DIM)
        kc = apply_partial_rope(kc.reshape(kvs), cos_n, sin_n, NSA_ROT_DIM)
        ksl = apply_partial_rope(ksl.reshape(kvs), cos_n, sin_n, NSA_ROT_DIM)
        kw = apply_partial_rope(kw.reshape(kvs), cos_n, sin_n, NSA_ROT_DIM)
        gates = jax.nn.sigmoid(ng.reshape(b, s, NSA_KV_GROUPS, NSA_GROUP, 3))
        o_nsa = nsa_attention(q, kc, vc.reshape(kvs), ksl, vsl.reshape(kvs), kw, vw.reshape(kvs), gates,
                              cmp_pe_k[l], cmp_pe_v[l], cmp_k_w1[l], cmp_k_b1[l], cmp_k_w2[l],
                              cmp_v_w1[l], cmp_v_b1[l], cmp_v_w2[l])
        y_a = o_nsa @ nsa_w_o[l]

        o_mla = mla_attention(cq, ckv, kr, mla_q_norm[l], mla_w_uq[l], mla_kv_norm[l], mla_w_uk[l], mla_w_uv[l],
                              cos_m, sin_m)
        y_b = o_mla @ mla_w_o[l]

        g = jax.nn.sigmoid(mg.reshape(b, s, 2, D_MODEL))
        mix = (g[:, :, 0] * y_a + g[:, :, 1] * y_b) @ w_out[l]
        x = layer_norm(DEEPNORM_ALPHA * x + mix, ln1_g[l], ln1_b[l])

        f = conv_glu(x, ffn_w_gate[l], ffn_w_up[l], ffn_conv_w[l], ffn_conv_b[l], ffn_w_down[l])
        x = layer_norm(DEEPNORM_ALPHA * x + f, ln2_g[l], ln2_b[l])
    return x
```

```python
import contextlib
import numpy as np
import ml_dtypes
import concourse.bass as bass
import concourse.mybir as mybir
from concourse.bass_utils import run_bass_kernel_spmd

F32 = mybir.dt.float32
BF16 = mybir.dt.bfloat16
AF = mybir.ActivationFunctionType
ALU = mybir.AluOpType

NCORES = 8
NSEQ = 2
S_ = 2048
DM = 1024
NT = 16
NCH = 4
DFF = 2816
NJ = 22
THETA = 500000.0
ALPHA = 2 ** 0.25
NEG = -30000.0
IN_W = 4408
O_NQ, O_KC, O_VC, O_KSL, O_VSL, O_KW, O_VW, O_NG, O_CQ, O_CKV, O_KR, O_MG = (
    0, 512, 640, 768, 896, 1024, 1152, 1280, 1304, 2072, 2328, 2360)


class Buf:
    def __init__(self, name):
        self.name = name
        self.last_write = None
        self.readers = {}
        self.excl = False


class TT:
    def __init__(self, t, b):
        self.t = t
        self.b = b


class Sched:
    def __init__(self, nc, stack):
        self.nc = nc
        self.stack = stack
        self.eng = {"pe": nc.tensor, "act": nc.scalar, "dve": nc.vector, "pool": nc.gpsimd, "sp": nc.sync}
        self.sem, self.cnt, self.seen = {}, {}, {}
        for k in self.eng:
            self.sem[k] = stack.enter_context(nc.semaphore("s_" + k))
            self.cnt[k] = 0
            self.seen[k] = {}
        self.dsem = {}
        self.halt = False

    def _wait(self, e, sem, val):
        seen = self.seen[e]
        key = id(sem)
        if seen.get(key, 0) >= val:
            return
        self.eng[e].wait_ge(sem, val)
        seen[key] = val

    def _deps(self, e, reads, writes, skip_sem=None):
        for b in reads:
            lw = b.last_write
            if lw is not None and not (e == "pe" and lw[2] == "pe"):
                self._wait(e, lw[0], lw[1])
            if b.excl:
                for re_, ev in b.readers.items():
                    if re_ != e:
                        self._wait(e, ev[0], ev[1])
        for b in writes:
            lw = b.last_write
            if lw is not None and not (e == "pe" and lw[2] == "pe") and not (skip_sem is not None and lw[0] is skip_sem):
                self._wait(e, lw[0], lw[1])
            for re_, ev in b.readers.items():
                if re_ == "pe" and e == "pe":
                    continue
                self._wait(e, ev[0], ev[1])

    def _post(self, e, ins, reads, writes, sig):
        ev = (self.sem[e], self.cnt[e] + 1, e)
        if sig:
            ins.then_inc(self.sem[e], 1)
            self.cnt[e] += 1
        else:
            assert e == "pe"
        for b in reads:
            b.readers[e] = ev
        for b in writes:
            b.last_write = ev
            b.readers = {}

    def mm(self, out, lhsT, rhs, start, stop, R, W, sig=True, skip=False):
        if self.halt:
            return
        self._deps("pe", R, W)
        ins = self.nc.tensor.matmul(out, lhsT=lhsT, rhs=rhs, start=start, stop=stop, skip_group_check=skip)
        self._post("pe", ins, R, W, sig)

    def tr(self, out, in_, ident, R, W, sig=True):
        if self.halt:
            return
        self._deps("pe", R, W)
        ins = self.nc.tensor.transpose(out=out, in_=in_, identity=ident)
        self._post("pe", ins, R, W, sig)

    def act(self, out, in_, func, R, W, **kw):
        if self.halt:
            return
        self._deps("act", R, W)
        ins = self.nc.scalar.activation(out=out, in_=in_, func=func, **kw)
        self._post("act", ins, R, W, True)

    def v(self, e, meth, R, W, **kw):
        if self.halt:
            return
        if e == "dve" and meth == "tensor_copy":
            meth = "tensor_scalar"
            kw = dict(out=kw["out"], in0=kw["in_"], scalar1=1.0, scalar2=None, op0=ALU.mult)
        self._deps(e, R, W)
        ins = getattr(self.eng[e], meth)(**kw)
        self._post(e, ins, R, W, True)

    def dma(self, q, out, in_, R, W, key):
        if self.halt:
            return
        if key.name not in self.dsem:
            self.dsem[key.name] = [self.stack.enter_context(self.nc.semaphore("d_" + key.name)), 0]
        ds = self.dsem[key.name]
        self._deps(q, R, W, skip_sem=ds[0])
        ins = self.eng[q].dma_start(out=out, in_=in_)
        ds[1] += 16
        ins.then_inc(ds[0], 16)
        ev = (ds[0], ds[1], "dma")
        for b in R:
            b.readers["dma_" + key.name] = ev
        for b in W:
            b.last_write = ev
            b.readers = {}

    def barrier(self):
        if self.halt:
            return
        for e in self.eng:
            for o in self.eng:
                if o != e:
                    self._wait(e, self.sem[o], self.cnt[o])
            for name, ds in self.dsem.items():
                if ds[1] > 0:
                    self._wait(e, ds[0], ds[1])


class Rot:
    def __init__(self, items):
        self.items = items
        self.i = 0

    def next(self):
        it = self.items[self.i % len(self.items)]
        self.i += 1
        return it


def _consts():
    bf = ml_dtypes.bfloat16
    pos = np.arange(S_, dtype=np.float32)
    invN = (THETA ** (-np.arange(0, 16, 2, dtype=np.float32) / 16)).astype(np.float32)
    angN = pos[None, :] * invN[:, None]
    cosN = np.ones((128, S_), np.float32)
    sinN = np.zeros((128, S_), np.float32)
    permN = np.zeros((128, 128), np.float32)
    for p in range(128):
        d = p % 64
        if d < 16:
            cosN[p] = np.cos(angN[d % 8])
            sinN[p] = -np.sin(angN[d]) if d < 8 else np.sin(angN[d - 8])
            src = p + 8 if d < 8 else p - 8
            permN[src, p] = 1.0
    invM = (THETA ** (-np.arange(0, 32, 2, dtype=np.float32) / 32)).astype(np.float32)
    angM = pos[None, :] * invM[:, None]
    cosM = np.ones((96, S_), np.float32)
    sinM = np.zeros((96, S_), np.float32)
    permM = np.zeros((96, 96), np.float32)
    for i in range(32):
        cosM[64 + i] = np.cos(angM[i % 16])
        sinM[64 + i] = -np.sin(angM[i]) if i < 16 else np.sin(angM[i - 16])
        permM[64 + (i + 16) % 32, 64 + i] = 1.0
    angT = angM.T.reshape(NT, 128, 16).transpose(1, 0, 2)
    cos2T = np.concatenate([np.cos(angT), np.cos(angT)], axis=2).astype(np.float32)
    sinST = np.concatenate([-np.sin(angT), np.sin(angT)], axis=2).astype(np.float32)
    n = np.arange(127)
    cmpmask = np.where((16 * n[:, None] + 31) <= pos[None, :].astype(np.int64), 0.0, NEG).astype(np.float32)
    cmpmask = np.concatenate([cmpmask, np.full((1, S_), NEG, np.float32)], axis=0)
    k = np.arange(128)
    tri = np.where(k[:, None] <= k[None, :], 0.0, NEG).astype(np.float32)
    tri2 = np.where(k[:, None] > k[None, :], 0.0, NEG).astype(np.float32)
    erow = np.zeros((96, S_), np.float32)
    for j in range(32):
        erow[64 + j, j * 64:(j + 1) * 64] = 1.0
    cs = n * 16
    ce = cs + 31
    ss = np.arange(32) * 64
    ovl = ((cs[:, None] <= ss[None, :] + 63) & (ce[:, None] >= ss[None, :])).astype(np.float32)
    vca = np.zeros((128, 2, 97), np.float32)
    vca[:, :, 64] = 1.0
    vca[:127, :, 65:97] = ovl[:, None, :]
    p = np.arange(S_)
    cur = p // 64
    j = np.arange(32)
    valid = j[None, :] <= cur[:, None]
    forced = ((j[None, :] == 0) | (j[None, :] == cur[:, None]) | (j[None, :] == cur[:, None] - 1)) & valid
    bonus = np.where(forced, 100.0, np.where(valid, 0.0, -1.0)).astype(np.float32)
    bonus = bonus.reshape(NT, 128, 32).transpose(1, 0, 2)
    ident = np.eye(128, dtype=np.float32)
    return {
        "c_cosN": cosN, "c_sinN": sinN, "c_permN": permN.astype(bf),
        "c_cosM": cosM, "c_sinM": sinM, "c_permM": permM.astype(bf),
        "c_cos2T": np.ascontiguousarray(cos2T), "c_sinST": np.ascontiguousarray(sinST),
        "c_cmpmask": cmpmask.astype(bf), "c_tri": tri.astype(bf), "c_tri2": tri2.astype(bf),
        "c_erow": erow.astype(bf), "c_vca": vca.astype(bf), "c_bonus": np.ascontiguousarray(bonus),
        "c_identf": ident, "c_identb": ident.astype(bf),
    }


_CONST_SHAPES = {
    "c_cosN": ([128, S_], F32), "c_sinN": ([128, S_], F32), "c_permN": ([128, 128], BF16),
    "c_cosM": ([96, S_], F32), "c_sinM": ([96, S_], F32), "c_permM": ([96, 96], BF16),
    "c_cos2T": ([128, NT, 32], F32), "c_sinST": ([128, NT, 32], F32),
    "c_cmpmask": ([128, S_], BF16), "c_tri": ([128, 128], BF16), "c_tri2": ([128, 128], BF16),
    "c_erow": ([96, S_], BF16), "c_vca": ([128, 2, 97], BF16), "c_bonus": ([128, NT, 32], F32),
    "c_identf": ([128, 128], F32), "c_identb": ([128, 128], BF16),
}

_W_SHAPES = {
    "w_in": [DM, IN_W], "pe_kT": [64, 32], "pe_vT": [64, 32],
    "cmp_k_w1": [2048, 256], "cmp_k_b1": [128, 2], "cmp_k_w2": [256, 64],
    "cmp_v_w1": [2048, 256], "cmp_v_b1": [128, 2], "cmp_v_w2": [256, 64],
    "nsa_w_o": [512, DM], "mla_q_norm": [1, 768], "mla_w_uq": [768, 768], "mla_kv_norm": [1, 256],
    "mla_w_uk": [256, 512], "mla_w_uv": [256, 512], "mla_w_o": [512, DM], "w_out": [DM, DM],
    "ln1_g": [1, DM], "ln1_b": [1, DM], "ffn_w_gate": [DM, DFF], "ffn_w_up": [DM, DFF],
    "ffn_conv_w": [128, 3, NJ], "ffn_conv_b": [128, NJ], "ffn_w_down": [DFF, DM], "ln2_g": [1, DM], "ln2_b": [1, DM],
}


class _Stop(Exception):
    pass


def build(nseq=NSEQ, debug=None, stop=None):
    nc = bass.Bass("TRN2", target_bir_lowering=False)
    dr = {}
    dr["xT"] = nc.dram_tensor("xT", [nseq, DM, S_], F32, kind="ExternalInput").ap()
    dr["x"] = nc.dram_tensor("x", [nseq, S_, DM], F32, kind="ExternalInput").ap()
    for k_, shp in _W_SHAPES.items():
        dr[k_] = nc.dram_tensor(k_, shp, F32, kind="ExternalInput").ap()
    for k_, (shp, dt) in _CONST_SHAPES.items():
        dr[k_] = nc.dram_tensor(k_, shp, dt, kind="ExternalInput").ap()
    out_d = nc.dram_tensor("out", [nseq, S_, DM], F32, kind="ExternalOutput").ap()
    dbg_d = {}
    if debug:
        for name, shp in debug.items():
            dbg_d[name] = nc.dram_tensor("dbg_" + name, shp, F32, kind="ExternalOutput").ap()

    with contextlib.ExitStack() as top:
        S = Sched(nc, top)

        uid = [0]

        def alloc(st, name, shape, dt=F32):
            uid[0] += 1
            return TT(st.enter_context(nc.sbuf_tensor("sb%d_%s" % (uid[0], name), shape, dt)), Buf(name))

        banks = [TT(top.enter_context(nc.psum_tensor("bank%d" % i, [128, 512], F32)), Buf("bank%d" % i)) for i in range(8)]
        for bk_ in banks:
            bk_.b.excl = True
        rotA = Rot(banks[0:3])
        rotB = Rot(banks[3:5])
        rotC = Rot(banks[5:8])

        def cload(st, name, q="sp"):
            shp, dt = _CONST_SHAPES[name]
            t = alloc(st, name, shp, dt)
            S.dma(q, t.t[:], dr[name], [], [t.b], t.b)
            return t

        identf = cload(top, "c_identf")
        identb = cload(top, "c_identb")
        permN = cload(top, "c_permN")
        permM = cload(top, "c_permM")
        tri = cload(top, "c_tri")
        tri2 = cload(top, "c_tri2")
        dbg_out = Buf("dbgout")

        def dump(name, src_ap, R):
            if name in dbg_d:
                S.dma("pool", dbg_d[name], src_ap, R, [], dbg_out)

        def wview(name, p=128):
            return dr[name].rearrange("(kc p) c -> p kc c", p=p)

        w_in_v = wview("w_in")
        outq = Buf("outq")

        def rope_store(st_tmp, bank, dest_ap, nrows, perm, cos_ap, sin_ap, Rtab, Wdest, tmps):
            qraw, t1, t2 = tmps.next()
            S.act(qraw.t[0:nrows, :], bank.t[0:nrows, :], AF.Copy, [bank.b], [qraw.b])
            rb = rotB.next()
            S.mm(rb.t[0:nrows, :], perm.t[0:nrows, 0:nrows], qraw.t[0:nrows, :], True, True, [perm.b, qraw.b], [rb.b])
            S.v("dve", "tensor_tensor", [rb.b] + Rtab, [t1.b], out=t1.t[0:nrows, :], in0=rb.t[0:nrows, :], in1=sin_ap, op=ALU.mult)
            S.v("dve", "tensor_tensor", [bank.b] + Rtab, [t2.b], out=t2.t[0:nrows, :], in0=bank.t[0:nrows, :], in1=cos_ap, op=ALU.mult)
            S.v("pool", "tensor_tensor", [t1.b, t2.b], Wdest, out=dest_ap, in0=t1.t[0:nrows, :], in1=t2.t[0:nrows, :], op=ALU.add)

        rotS = Rot(banks[0:3] + [banks[5]])

        def attn(c, tiles, kT_fn, q_fn, v_fn, W, scale, nk, acc, ptrot, jobs, epi=None):
            lastt = {}
            for (t, b0, b1, masks) in tiles:
                for b in range(b0, b1):
                    lastt[b] = t
            for idx, (t, b0, b1, masks) in enumerate(tiles):
                jobs.append(dict(c=c, t=t, b0=b0, b1=b1, masks=masks, k=kT_fn(t), q=q_fn(c * 512 + b0 * 128, (b1 - b0) * 128),
                                 v=v_fn(t), W=W, scale=scale, nk=nk, acc=acc, first=(idx == 0), lastt=lastt, ptrot=ptrot,
                                 epi=(epi if idx == len(tiles) - 1 else None)))

        def job_score(j):
            n = (j["b1"] - j["b0"]) * 128
            nk = j["nk"]
            sb = rotS.next()
            j["sb"] = sb
            kap, kb = j["k"]
            qap, qb = j["q"]
            masks = j["masks"]
            S.mm(sb.t[0:nk, 0:n], kap, qap, True, not masks, kb + qb, [sb.b], sig=not masks)
            for i, (c0, ncol, map_, mb) in enumerate(masks):
                last = i == len(masks) - 1
                S.mm(sb.t[0:nk, c0:c0 + ncol], identb.t[0:nk, 0:nk], map_, False, last, [identb.b, mb], [sb.b], sig=last)

        def job_rest(j):
            n = (j["b1"] - j["b0"]) * 128
            nk, W, acc, sb, b0, b1 = j["nk"], j["W"], j["acc"], j["sb"], j["b0"], j["b1"]
            pt = j["ptrot"].next()
            S.act(pt.t[0:nk, 0:n], sb.t[0:nk, 0:n], AF.Exp, [sb.b], [pt.b], scale=j["scale"])
            vap, vb = j["v"]
            for b in range(b0, b1):
                S.mm(acc.t[:, b * W:(b + 1) * W], pt.t[0:nk, (b - b0) * 128:(b - b0 + 1) * 128], vap,
                     j["first"] and b == b0, j["lastt"][b] == j["t"], [pt.b] + vb, [acc.b], sig=(b == b1 - 1), skip=True)
            if j["epi"] is not None:
                j["epi"]()

        def run_jobs(jobs, L=3):
            n = len(jobs)
            si = 0
            ahead = 0
            for idx, j in enumerate(jobs):
                if "hook" in j:
                    j["hook"]()
                    if si <= idx:
                        si = idx + 1
                    continue
                if si < idx:
                    si = idx
                while si < n and ahead <= L:
                    js = jobs[si]
                    if "hook" in js:
                        if js.get("block"):
                            break
                        si += 1
                        continue
                    job_score(js)
                    ahead += 1
                    si += 1
                job_rest(j)
                ahead -= 1

        def layer_norm(r, gt, bt, out_ap, outbuf, tmp):
            stats, mv, sc = tmp
            for hh in range(2):
                S.v("dve", "bn_stats", [r.b], [stats.b], out=stats.t[:, hh, :], in_=r.t[:, hh * 512:(hh + 1) * 512])
            S.v("dve", "bn_aggr", [stats.b], [mv.b], out=mv.t[:, 0:2], in_=stats.t[:].rearrange("p a b -> p (a b)"))
            S.v("dve", "tensor_scalar", [mv.b], [sc.b], out=sc.t[:, 0:1], in0=mv.t[:, 1:2], scalar1=1e-5, scalar2=None, op0=ALU.add)
            S.act(sc.t[:, 1:2], sc.t[:, 0:1], AF.Ln, [sc.b], [sc.b])
            S.act(sc.t[:, 2:3], sc.t[:, 1:2], AF.Exp, [sc.b], [sc.b], scale=-0.5)
            S.v("dve", "tensor_scalar", [r.b, mv.b, sc.b], [r.b], out=r.t[:], in0=r.t[:], scalar1=mv.t[:, 0:1], scalar2=sc.t[:, 2:3],
                op0=ALU.subtract, op1=ALU.mult)
            S.v("dve", "tensor_tensor", [r.b, gt.b], [r.b], out=r.t[:], in0=r.t[:], in1=gt.t[:], op=ALU.mult)
            if out_ap is not None:
                ln_bias(r, bt, out_ap, outbuf)

        def ln_bias(r, bt, out_ap, outbuf):
            S.v("dve", "tensor_tensor", [r.b, bt.b], [outbuf], out=out_ap, in0=r.t[:], in1=bt.t[:], op=ALU.add)

        def checkpoint(name):
            if stop == name:
                S.barrier()
                S.halt = True

        checkpoint("init")
        for s in range(nseq):
          with contextlib.ExitStack() as seqst:
            mergedT = alloc(seqst, "mergedT", [128, 8, S_], BF16)
            with contextlib.ExitStack() as mixst:
                onsaT = alloc(mixst, "onsaT", [128, 4, S_], BF16)
                omlaT = alloc(mixst, "omlaT", [128, 4, S_], BF16)

                def load_xT(st):
                    xT = alloc(st, "xT_bf", [128, 8, S_], BF16)
                    xsrc = dr["xT"][s].rearrange("(kc p) t -> p kc t", p=128)
                    for hf_ in range(2):
                        S.dma("pool", xT.t[:, :, hf_ * 1024:(hf_ + 1) * 1024], xsrc[:, :, hf_ * 1024:(hf_ + 1) * 1024], [], [xT.b], xT.b)
                    return xT

                with contextlib.ExitStack() as nsast:
                    xT = load_xT(nsast)
                    cmpmask = cload(nsast, "c_cmpmask")
                    bonus = cload(nsast, "c_bonus")
                    qpair = [alloc(nsast, "qpair%d" % i, [128, S_], BF16) for i in range(4)]
                    kcpair = alloc(nsast, "kcpair", [128, S_], BF16)
                    vcpair = alloc(nsast, "vcpair", [128, S_], BF16)
                    kwpair = alloc(nsast, "kwpair", [128, S_], BF16)
                    kslpair = alloc(nsast, "kslpair", [128, S_], BF16)
                    ksel = [alloc(nsast, "ksel%d" % g, [96, S_], BF16) for g in range(2)]
                    V1sel = alloc(nsast, "V1sel", [128, NT, 2, 66], BF16)
                    V1win = alloc(nsast, "V1win", [128, NT, 2, 66], BF16)
                    gates = alloc(nsast, "gates", [128, NT, 24], F32)
                    kcT = alloc(nsast, "kcT", [128, 128], BF16)
                    vca = alloc(nsast, "vca", [128, 2, 97], BF16)
                    S.dma("sp", vca.t[:], dr["c_vca"], [], [vca.b], vca.b)
                    for g in range(2):
                        S.dma("sp", ksel[g].t[64:96, :], dr["c_erow"][64:96, :], [], [ksel[g].b], ksel[g].b)
                    S.v("pool", "memset", [], [V1sel.b], ap=V1sel.t[:, :, :, 64:65], constant=1.0)
                    S.v("pool", "memset", [], [V1win.b], ap=V1win.t[:, :, :, 64:65], constant=1.0)

                    wtm = alloc(nsast, "wtm", [128, 8, 280], BF16)
                    egt = alloc(nsast, "egt", [128, 24], F32)
                    with contextlib.ExitStack() as pst:
                        cosN = cload(pst, "c_cosN")
                        sinN = cload(pst, "c_sinN")
                        wbufs = Rot([alloc(pst, "wbuf%d" % i, [128, 8, 128], BF16) for i in range(8)])
                        tmps = Rot([(alloc(pst, "qraw%d" % i, [128, 512], BF16), alloc(pst, "rt1_%d" % i, [128, 512], F32),
                                     alloc(pst, "rt2_%d" % i, [128, 512], F32)) for i in range(2)])
                        groups = []
                        for p_ in range(4):
                            groups.append(([(p_ * 64, 64), ((4 + p_) * 64, 64)], qpair[p_], True))
                        groups.append(([(O_KC, 128)], kcpair, True))
                        groups.append(([(O_VC, 128)], vcpair, False))
                        groups.append(([(O_KSL, 128)], kslpair, True))
                        groups.append(([(O_KW, 128)], kwpair, True))
                        gw = []
                        for (cols, dest, rope) in groups:
                            wb = wbufs.next()
                            d0 = 0
                            for (c0, n) in cols:
                                S.dma("pool", wb.t[:, :, d0:d0 + n], w_in_v[:, :, c0:c0 + n], [], [wb.b], wb.b)
                                d0 += n
                            gw.append(wb)
                        for (c0, n, d0) in ((O_VSL, 128, 0), (O_VW, 128, 128), (O_NG, 24, 256)):
                            S.dma("pool", wtm.t[:, :, d0:d0 + n], w_in_v[:, :, c0:c0 + n], [], [wtm.b], wtm.b)
                        for gi, (cols, dest, rope) in enumerate(groups):
                            wb = gw[gi]
                            for c in range(NCH):
                                bk = rotA.next()
                                for kc in range(8):
                                    S.mm(bk.t[:, :], wb.t[:, kc, :], xT.t[:, kc, c * 512:(c + 1) * 512], kc == 0, kc == 7,
                                         [wb.b, xT.b], [bk.b], sig=(kc == 7))
                                dst = dest.t[:, c * 512:(c + 1) * 512]
                                if rope:
                                    rope_store(pst, bk, dst, 128, permN, cosN.t[:, c * 512:(c + 1) * 512],
                                               sinN.t[:, c * 512:(c + 1) * 512], [cosN.b, sinN.b], [dest.b], tmps)
                                else:
                                    S.act(dst, bk.t[:, :], AF.Copy, [bk.b], [dest.b])
                        checkpoint("fm")
                        S.v("pool", "tensor_copy", [kslpair.b], [ksel[0].b], out=ksel[0].t[0:64, :], in_=kslpair.t[0:64, :])
                        S.act(ksel[1].t[0:64, :], kslpair.t[64:128, :], AF.Copy, [kslpair.b], [ksel[1].b])
                        checkpoint("ksel")
                        S.barrier()
                    if s == 0:
                        checkpoint("pre")
                        dump("qpair0", qpair[0].t[:], [qpair[0].b])
                        dump("kcpair", kcpair.t[:], [kcpair.b])
                        dump("gates", gates.t[:].rearrange("p a b -> p (a b)"), [gates.b])
                        dump("V1sel", V1sel.t[:].rearrange("p a b c -> p (a b c)"), [V1sel.b])
                        checkpoint("proj")

                    with contextlib.ExitStack() as cst:
                        w1s = {"k": alloc(cst, "w1k", [128, 32, 256], BF16), "v": alloc(cst, "w1v", [128, 32, 256], BF16)}
                        for typ in ("k", "v"):
                            w1v = dr["cmp_%s_w1" % typ].rearrange("(l d) h -> d l h", d=64)
                            for g in range(2):
                                S.dma("pool", w1s[typ].t[g * 64:(g + 1) * 64, :, :], w1v, [], [w1s[typ].b], w1s[typ].b)
                        for t in range(NT):
                            bk = rotA.next()
                            for kc in range(8):
                                S.mm(bk.t[:, 0:280], xT.t[:, kc, t * 128:(t + 1) * 128], wtm.t[:, kc, :], kc == 0, kc == 7,
                                     [wtm.b, xT.b], [bk.b], sig=(kc == 7))
                            checkpoint("tm_mm")
                            S.act(V1sel.t[:, t, :, 0:64], bk.t[:, 0:128].rearrange("p (g d) -> p g d", g=2), AF.Copy, [bk.b], [V1sel.b])
                            checkpoint("tm_v1")
                            S.v("dve", "tensor_copy", [bk.b], [V1win.b], out=V1win.t[:, t, :, 0:64],
                                in_=bk.t[:, 128:256].rearrange("p (g d) -> p g d", g=2))
                            checkpoint("tm_v2")
                            S.act(egt.t[:], bk.t[:, 256:280], AF.Exp, [bk.b], [egt.b], scale=-1.0)
                            checkpoint("tm_e")
                            S.v("dve", "tensor_scalar", [egt.b], [egt.b], out=egt.t[:], in0=egt.t[:], scalar1=1.0, scalar2=None, op0=ALU.add)
                            S.v("dve", "reciprocal", [egt.b], [gates.b], out=gates.t[:, t, :], in_=egt.t[:])
                        w2 = alloc(cst, "w2", [128, 2, 64], BF16)
                        b1 = alloc(cst, "b1", [128, 2], F32)
                        peT = alloc(cst, "peT", [64, 32], BF16)
                        biasc = alloc(cst, "biasc", [128, 2], F32)
                        H1g = alloc(cst, "H1g", [128, 2, 2, 128], BF16)
                        for typ in ("k", "v"):
                            src = kcpair if typ == "k" else vcpair
                            w1 = w1s[typ]
                            S.dma("pool", w2.t[:], dr["cmp_%s_w2" % typ].rearrange("(hh p) d -> p hh d", p=128), [], [w2.b], w2.b)
                            S.dma("sp", b1.t[:], dr["cmp_%s_b1" % typ], [], [b1.b], b1.b)
                            S.dma("pool", peT.t[:], dr["pe_%sT" % typ], [], [peT.b], peT.b)
                            bb = rotC.next()
                            for hh in range(2):
                                for l in range(32):
                                    S.mm(bb.t[:, hh:hh + 1], w1.t[0:64, l, hh * 128:(hh + 1) * 128], peT.t[0:64, l:l + 1],
                                         l == 0, l == 31, [w1.b, peT.b], [bb.b], sig=(l == 31))
                            S.v("dve", "tensor_tensor", [bb.b, b1.b], [biasc.b], out=biasc.t[:], in0=bb.t[:, 0:2], in1=b1.t[:], op=ALU.add)
                            for g in range(2):
                                for hh in range(2):
                                    hb = rotA.next()
                                    for l in range(32):
                                        S.mm(hb.t[:, 0:127], w1.t[g * 64:(g + 1) * 64, l, hh * 128:(hh + 1) * 128],
                                             src.t[g * 64:(g + 1) * 64, l:l + 2017:16], l == 0, l == 31, [w1.b, src.b], [hb.b], sig=(l == 31))
                                    S.act(H1g.t[:, g, hh, 0:127], hb.t[:, 0:127], AF.Gelu_apprx_tanh, [hb.b, biasc.b], [H1g.b],
                                          bias=biasc.t[:, hh:hh + 1])
                            ob = rotB.next()
                            if typ == "k":
                                for g in range(2):
                                    for hh in range(2):
                                        S.mm(ob.t[g * 64:(g + 1) * 64, 0:127], w2.t[:, hh, :], H1g.t[:, g, hh, 0:127], hh == 0, hh == 1,
                                             [w2.b, H1g.b], [ob.b], sig=(hh == 1))
                                S.act(kcT.t[:, 0:127], ob.t[:, 0:127], AF.Copy, [ob.b], [kcT.b])
                            else:
                                for g in range(2):
                                    for hh in range(2):
                                        S.mm(ob.t[0:127, g * 64:(g + 1) * 64], H1g.t[:, g, hh, 0:127], w2.t[:, hh, :], hh == 0, hh == 1,
                                             [w2.b, H1g.b], [ob.b], sig=(hh == 1))
                                S.act(vca.t[0:127, :, 0:64], ob.t[0:127, 0:128].rearrange("p (g d) -> p g d", g=2), AF.Copy, [ob.b], [vca.b])
                        S.barrier()
                    if s == 0:
                        dump("kcT", kcT.t[:], [kcT.b])
                        dump("vca", vca.t[:].rearrange("p a b -> p (a b)"), [vca.b])
                        checkpoint("cmp")

                    with contextlib.ExitStack() as ast:
                        qaug = [alloc(ast, "qaug%d" % i, [96, S_], BF16) for i in range(4)]
                        ptrot = Rot([alloc(ast, "pt%d" % i, [128, 512], BF16) for i in range(3)])
                        ochunks = Rot([alloc(ast, "och%d" % i, [128, 4, 256], F32) for i in range(2)])
                        otmps = Rot([alloc(ast, "otmp%d" % i, [128, 4, 64], F32) for i in range(2)])
                        impacc = alloc(ast, "impacc", [128, 4, 32], F32)
                        itmps = Rot([alloc(ast, "itmp%d" % i, [128, 4, 32], F32) for i in range(2)])
                        rzs = Rot([alloc(ast, "rz%d" % i, [128, 8], F32) for i in range(3)])
                        top8 = alloc(ast, "top8", [128, 4, 8], F32)
                        negm = alloc(ast, "negm", [128, 4, 96], F32)
                        S.v("pool", "memset", [], [negm.b], ap=negm.t[:], constant=0.0)
                        def nsa_epi(acc, W, i, br, first, with_imp, g, c, och):
                            rz = rzs.next()
                            accv = acc.t[:, 0:4 * W].rearrange("p (b w) -> p b w", b=4)
                            S.v("dve", "tensor_scalar", [acc.b], [rz.b], out=rz.t[:, 0:4].rearrange("p (b o) -> p b o", o=1),
                                in0=accv[:, :, 64:65], scalar1=1e-30, scalar2=None, op0=ALU.max)
                            S.v("dve", "reciprocal", [rz.b], [rz.b], out=rz.t[:, 0:4], in_=rz.t[:, 0:4])
                            h = 4 * g + i
                            S.v("dve", "tensor_tensor", [rz.b, gates.b], [rz.b], out=rz.t[:, 4:8].rearrange("p (b o) -> p b o", o=1),
                                in0=rz.t[:, 0:4].rearrange("p (b o) -> p b o", o=1),
                                in1=gates.t[:, 4 * c:4 * c + 4, h * 3 + br:h * 3 + br + 1], op=ALU.mult)
                            gs_bc = rz.t[:, 4:8].rearrange("p (b o) -> p b o", o=1).broadcast_to([128, 4, 64])
                            if first:
                                S.v("dve", "tensor_tensor", [acc.b, rz.b], [och.b], out=och.t[:, :, i * 64:(i + 1) * 64],
                                    in0=accv[:, :, 0:64], in1=gs_bc, op=ALU.mult)
                            else:
                                ot = otmps.next()
                                S.v("dve", "tensor_tensor", [acc.b, rz.b], [ot.b], out=ot.t[:], in0=accv[:, :, 0:64], in1=gs_bc, op=ALU.mult)
                                S.v("pool", "tensor_tensor", [ot.b, och.b], [och.b], out=och.t[:, :, i * 64:(i + 1) * 64],
                                    in0=och.t[:, :, i * 64:(i + 1) * 64], in1=ot.t[:], op=ALU.add)
                            if with_imp:
                                it = itmps.next()
                                rz_bc = rz.t[:, 0:4].rearrange("p (b o) -> p b o", o=1).broadcast_to([128, 4, 32])
                                S.v("dve", "tensor_tensor", [acc.b, rz.b], [it.b], out=it.t[:], in0=accv[:, :, 65:97], in1=rz_bc, op=ALU.mult)
                                S.v("pool", "tensor_tensor", [it.b, impacc.b], [impacc.b], out=impacc.t[:], in0=impacc.t[:], in1=it.t[:], op=ALU.add)

                        def build_cmp(g, c, och, jobs):
                            pb = g * 64
                            for i in range(4):
                                acc = rotB.next()
                                attn(c, [(0, 0, 4, [(0, 512, cmpmask.t[0:127, c * 512:(c + 1) * 512], cmpmask.b)])],
                                     lambda t: (kcT.t[pb:pb + 64, 0:127], [kcT.b]),
                                     lambda c0, n, i=i: (qpair[i].t[pb:pb + 64, c0:c0 + n], [qpair[i].b]),
                                     lambda t: (vca.t[0:127, g, :], [vca.b]), 97, 0.125, 127, acc, ptrot, jobs,
                                     epi=(lambda acc=acc, i=i: nsa_epi(acc, 97, i, 0, True, True, g, c, och)))

                        def build_win(g, c, och, jobs):
                            pb = g * 64
                            for i in range(4):
                                acc = rotB.next()
                                tiles = []
                                for t in range(max(0, 4 * c - 2), 4 * c + 4):
                                    b0 = max(t, 4 * c) - 4 * c
                                    b1 = min(t + 2, 4 * c + 3) - 4 * c + 1
                                    masks = []
                                    if t >= 4 * c:
                                        masks.append((0, 128, tri.t[:], tri.b))
                                    if t + 2 <= 4 * c + 3:
                                        masks.append(((t + 2 - 4 * c - b0) * 128, 128, tri2.t[:], tri2.b))
                                    tiles.append((t, b0, b1, masks))
                                attn(c, tiles,
                                     lambda t: (kwpair.t[pb:pb + 64, t * 128:(t + 1) * 128], [kwpair.b]),
                                     lambda c0, n, i=i: (qpair[i].t[pb:pb + 64, c0:c0 + n], [qpair[i].b]),
                                     lambda t: (V1win.t[:, t, g, 0:65], [V1win.b]), 65, 0.125, 128, acc, ptrot, jobs,
                                     epi=(lambda acc=acc, i=i: nsa_epi(acc, 65, i, 2, False, False, g, c, och)))

                        def build_sel(g, c, och, jobs):
                            for i in range(4):
                                acc = rotB.next()
                                tiles = []
                                for t in range(4 * c + 4):
                                    if t < 4 * c:
                                        tiles.append((t, 0, 4, []))
                                    else:
                                        tiles.append((t, t - 4 * c, 4, [(0, 128, tri.t[:], tri.b)]))
                                attn(c, tiles,
                                     lambda t: (ksel[g].t[0:96, t * 128:(t + 1) * 128], [ksel[g].b]),
                                     lambda c0, n, i=i: (qaug[i].t[0:96, c0:c0 + n], [qaug[i].b]),
                                     lambda t: (V1sel.t[:, t, g, 0:65], [V1sel.b]), 65, 0.125, 128, acc, ptrot, jobs,
                                     epi=(lambda acc=acc, i=i: nsa_epi(acc, 65, i, 1, False, False, g, c, och)))

                        def hook_init(c):
                            S.v("pool", "tensor_copy", [bonus.b], [impacc.b], out=impacc.t[:], in_=bonus.t[:, 4 * c:4 * c + 4, :])

                        def hook_topk():
                            for b in range(4):
                                S.v("dve", "max", [impacc.b], [top8.b], out=top8.t[:, b, :], in_=impacc.t[:, b, :])
                            S.v("dve", "tensor_tensor", [impacc.b, top8.b], [negm.b], out=negm.t[:, :, 64:96], in0=impacc.t[:],
                                in1=top8.t[:, :, 7:8].broadcast_to([128, 4, 32]), op=ALU.is_lt)
                            S.v("dve", "tensor_scalar", [negm.b], [negm.b], out=negm.t[:, :, 64:96], in0=negm.t[:, :, 64:96],
                                scalar1=NEG, scalar2=None, op0=ALU.mult)

                        def hook_neg(c):
                            nb = rotC.next()
                            for b in range(4):
                                S.tr(nb.t[0:96, b * 128:(b + 1) * 128], negm.t[:, b, :], identf.t[:], [negm.b, identf.b], [nb.b], sig=(b == 3))
                            for i in range(4):
                                if i % 2 == 0:
                                    S.act(qaug[i].t[64:96, c * 512:(c + 1) * 512], nb.t[64:96, :], AF.Copy, [nb.b], [qaug[i].b])
                                else:
                                    S.v("dve", "tensor_copy", [nb.b], [qaug[i].b], out=qaug[i].t[64:96, c * 512:(c + 1) * 512], in_=nb.t[64:96, :])

                        def hook_och(g, c, och):
                            for j in range(2):
                                tb = rotC.next()
                                for b in range(4):
                                    S.tr(tb.t[:, b * 128:(b + 1) * 128], och.t[:, b, j * 128:(j + 1) * 128], identf.t[:],
                                         [och.b, identf.b], [tb.b], sig=(b == 3))
                                if j == 0:
                                    S.act(onsaT.t[:, 2 * g + j, c * 512:(c + 1) * 512], tb.t[:, :], AF.Copy, [tb.b], [onsaT.b])
                                else:
                                    S.v("dve", "tensor_copy", [tb.b], [onsaT.b], out=onsaT.t[:, 2 * g + j, c * 512:(c + 1) * 512], in_=tb.t[:, :])

                        for g in range(2):
                            for i in range(4):
                                if g == 0:
                                    S.v("pool", "tensor_copy", [qpair[i].b], [qaug[i].b], out=qaug[i].t[0:64, :], in_=qpair[i].t[0:64, :])
                                else:
                                    S.act(qaug[i].t[0:64, :], qpair[i].t[64:128, :], AF.Copy, [qpair[i].b], [qaug[i].b])
                            jobs = []
                            och = {0: ochunks.next()}
                            jobs.append(dict(hook=(lambda: hook_init(0))))
                            build_cmp(g, 0, och[0], jobs)
                            jobs.append(dict(hook=hook_topk))
                            build_win(g, 0, och[0], jobs)
                            jobs.append(dict(hook=(lambda: hook_neg(0)), block=True))
                            for c in range(NCH):
                                if c + 1 < NCH:
                                    och[c + 1] = ochunks.next()
                                    jobs.append(dict(hook=(lambda c=c: hook_init(c + 1))))
                                    build_cmp(g, c + 1, och[c + 1], jobs)
                                    jobs.append(dict(hook=hook_topk))
                                build_sel(g, c, och[c], jobs)
                                if c + 1 < NCH:
                                    build_win(g, c + 1, och[c + 1], jobs)
                                    jobs.append(dict(hook=(lambda c=c: hook_neg(c + 1)), block=True))
                                jobs.append(dict(hook=(lambda c=c, g=g, o=och[c]: hook_och(g, c, o))))
                            run_jobs(jobs)
                        S.barrier()
                    if s == 0:
                        dump("onsaT", onsaT.t[:].rearrange("p a b -> p (a b)"), [onsaT.b])
                        checkpoint("nsa")
                    S.barrier()

                with contextlib.ExitStack() as mlast:
                    cqnT = alloc(mlast, "cqnT", [128, 6, S_], BF16)
                    cnT = alloc(mlast, "cnT", [128, 2, S_], BF16)
                    kropeT = alloc(mlast, "kropeT", [96, S_], BF16)
                    wuq = alloc(mlast, "wuq", [128, 6, 768], BF16)
                    wuk = alloc(mlast, "wuk", [128, 2, 512], BF16)
                    wuv = alloc(mlast, "wuv", [128, 2, 512], BF16)
                    with contextlib.ExitStack() as p1:
                        wcq = alloc(p1, "wcq", [128, 8, 768], BF16)
                        wckv = alloc(p1, "wckv", [128, 8, 288], BF16)
                        S.dma("pool", wcq.t[:], w_in_v[:, :, O_CQ:O_CQ + 768], [], [wcq.b], wcq.b)
                        S.dma("pool", wckv.t[:], w_in_v[:, :, O_CKV:O_CKV + 288], [], [wckv.b], wckv.b)
                        xT = load_xT(p1)
                        S.dma("pool", wuq.t[:], wview("mla_w_uq"), [], [wuq.b], wuq.b)
                        S.dma("pool", wuk.t[:], wview("mla_w_uk"), [], [wuk.b], wuk.b)
                        S.dma("pool", wuv.t[:], wview("mla_w_uv"), [], [wuv.b], wuv.b)
                        gq = alloc(p1, "gq", [128, 768], F32)
                        gkv = alloc(p1, "gkv", [128, 256], F32)
                        S.dma("sp", gq.t[:], dr["mla_q_norm"].partition_broadcast(128)[:, 0, :], [], [gq.b], gq.b)
                        S.dma("sp", gkv.t[:], dr["mla_kv_norm"].partition_broadcast(128)[:, 0, :], [], [gkv.b], gkv.b)
                        cos2T = cload(p1, "c_cos2T")
                        sinST = cload(p1, "c_sinST")
                        cqns = Rot([alloc(p1, "cqn%d" % i, [128, 1024], F32) for i in range(2)])
                        krs = alloc(p1, "krs", [128, 96], F32)
                        S.v("pool", "memset", [], [krs.b], ap=krs.t[:], constant=0.0)
                        junk = alloc(p1, "junk", [128, 512], BF16)
                        sst = Rot([alloc(p1, "sst%d" % i, [128, 8], F32) for i in range(2)])
                        krt = alloc(p1, "krt", [128, 64], F32)
                        rot6 = Rot(banks[0:6])
                        rot2 = Rot(banks[6:8])

                        def p1_mm(t):
                            bA, bB, bC = rot6.next(), rot6.next(), rot6.next()
                            for (bk, wt, c0, n) in ((bA, wcq, 0, 512), (bB, wcq, 512, 256), (bC, wckv, 0, 288)):
                                for kc in range(8):
                                    S.mm(bk.t[:, 0:n], xT.t[:, kc, t * 128:(t + 1) * 128], wt.t[:, kc, c0:c0 + n], kc == 0, kc == 7,
                                         [wt.b, xT.b], [bk.b], sig=(kc == 7))
                            return (bA, bB, bC)

                        def p1_rest(t, bks):
                            bA, bB, bC = bks
                            ss = sst.next()
                            S.act(junk.t[:, 0:512], bA.t[:, 0:512], AF.Square, [bA.b], [junk.b, ss.b], accum_out=ss.t[:, 0:1])
                            S.act(junk.t[:, 0:256], bB.t[:, 0:256], AF.Square, [bB.b], [junk.b, ss.b], accum_out=ss.t[:, 1:2])
                            S.act(junk.t[:, 0:256], bC.t[:, 0:256], AF.Square, [bC.b], [junk.b, ss.b], accum_out=ss.t[:, 2:3])
                            S.v("dve", "tensor_tensor", [ss.b], [ss.b], out=ss.t[:, 3:4], in0=ss.t[:, 0:1], in1=ss.t[:, 1:2], op=ALU.add)
                            S.v("dve", "tensor_scalar", [ss.b], [ss.b], out=ss.t[:, 4:5], in0=ss.t[:, 3:4], scalar1=1.0 / 768, scalar2=1e-6,
                                op0=ALU.mult, op1=ALU.add)
                            S.v("dve", "tensor_scalar", [ss.b], [ss.b], out=ss.t[:, 5:6], in0=ss.t[:, 2:3], scalar1=1.0 / 256, scalar2=1e-6,
                                op0=ALU.mult, op1=ALU.add)
                            S.act(ss.t[:, 6:8], ss.t[:, 4:6], AF.Ln, [ss.b], [ss.b])
                            S.act(ss.t[:, 4:6], ss.t[:, 6:8], AF.Exp, [ss.b], [ss.b], scale=-0.5)
                            cqn = cqns.next()
                            S.v("dve", "scalar_tensor_tensor", [bA.b, ss.b, gq.b], [cqn.b], out=cqn.t[:, 0:512], in0=bA.t[:, 0:512],
                                scalar=ss.t[:, 4:5], in1=gq.t[:, 0:512], op0=ALU.mult, op1=ALU.mult)
                            S.v("dve", "scalar_tensor_tensor", [bB.b, ss.b, gq.b], [cqn.b], out=cqn.t[:, 512:768], in0=bB.t[:, 0:256],
                                scalar=ss.t[:, 4:5], in1=gq.t[:, 512:768], op0=ALU.mult, op1=ALU.mult)
                            S.v("dve", "scalar_tensor_tensor", [bC.b, ss.b, gkv.b], [cqn.b], out=cqn.t[:, 768:1024], in0=bC.t[:, 0:256],
                                scalar=ss.t[:, 5:6], in1=gkv.t[:, :], op0=ALU.mult, op1=ALU.mult)
                            S.v("dve", "tensor_tensor", [bC.b, sinST.b], [krt.b], out=krt.t[:, 0:16], in0=bC.t[:, 272:288], in1=sinST.t[:, t, 0:16], op=ALU.mult)
                            S.v("dve", "tensor_tensor", [bC.b, sinST.b], [krt.b], out=krt.t[:, 16:32], in0=bC.t[:, 256:272], in1=sinST.t[:, t, 16:32], op=ALU.mult)
                            S.v("dve", "tensor_tensor", [bC.b, cos2T.b], [krt.b], out=krt.t[:, 32:64], in0=bC.t[:, 256:288], in1=cos2T.t[:, t, :], op=ALU.mult)
                            S.v("dve", "tensor_tensor", [krt.b], [krs.b], out=krs.t[:, 64:96], in0=krt.t[:, 0:32], in1=krt.t[:, 32:64], op=ALU.add)
                            tA, tB = rot2.next(), rot2.next()
                            for k_ in range(4):
                                S.tr(tA.t[:, k_ * 128:(k_ + 1) * 128], cqn.t[:, k_ * 128:(k_ + 1) * 128], identf.t[:], [cqn.b, identf.b], [tA.b], sig=(k_ == 3))
                            for k_ in range(4):
                                S.tr(tB.t[:, k_ * 128:(k_ + 1) * 128], cqn.t[:, (4 + k_) * 128:(5 + k_) * 128], identf.t[:], [cqn.b, identf.b], [tB.b], sig=(k_ == 3))
                            ts_ = slice(t * 128, (t + 1) * 128)
                            S.act(cqnT.t[:, 0:4, ts_], tA.t[:, :].rearrange("p (k n) -> p k n", k=4), AF.Copy, [tA.b], [cqnT.b])
                            S.v("dve", "tensor_copy", [tB.b], [cqnT.b], out=cqnT.t[:, 4:6, ts_], in_=tB.t[:, 0:256].rearrange("p (k n) -> p k n", k=2))
                            S.act(cnT.t[:, 0:2, ts_], tB.t[:, 256:512].rearrange("p (k n) -> p k n", k=2), AF.Copy, [tB.b], [cnT.b])
                            tK = rot2.next()
                            S.tr(tK.t[0:96, 0:128], krs.t[:, :], identf.t[:], [krs.b, identf.b], [tK.b])
                            S.v("dve", "tensor_copy", [tK.b], [kropeT.b], out=kropeT.t[64:96, ts_], in_=tK.t[64:96, 0:128])

                        pend = p1_mm(0)
                        for t in range(NT):
                            nxt = p1_mm(t + 1) if t + 1 < NT else None
                            p1_rest(t, pend)
                            pend = nxt
                        S.barrier()
                    if s == 0:
                        dump("cqnT", cqnT.t[:].rearrange("p a b -> p (a b)"), [cqnT.b])
                        dump("kropeT", kropeT.t[64:96, :], [kropeT.b])
                        checkpoint("mla1")
                        S.barrier()

                    with contextlib.ExitStack() as p2:
                        cosM = cload(p2, "c_cosM")
                        sinM = cload(p2, "c_sinM")
                        qmla = [alloc(p2, "qmla%d" % i, [96, S_], BF16) for i in range(4)]
                        kaug = [alloc(p2, "kaug%d" % i, [96, S_], BF16) for i in range(4)]
                        V1m = alloc(p2, "V1m", [128, NT, 4, 66], BF16)
                        S.v("pool", "memset", [], [V1m.b], ap=V1m.t[:, :, :, 64:65], constant=1.0)
                        tmps = Rot([(alloc(p2, "mqraw%d" % i, [128, 512], BF16), alloc(p2, "mrt1_%d" % i, [128, 512], F32),
                                     alloc(p2, "mrt2_%d" % i, [128, 512], F32)) for i in range(2)])
                        ptrot = Rot([alloc(p2, "mpt%d" % i, [128, 512], BF16) for i in range(3)])
                        ochunks = Rot([alloc(p2, "moch%d" % i, [128, 4, 256], F32) for i in range(2)])
                        rzs = Rot([alloc(p2, "mrz%d" % i, [128, 4], F32) for i in range(3)])
                        for hf in range(2):
                            for i in range(4):
                                h = 4 * hf + i
                                for c in range(NCH):
                                    bk = rotA.next()
                                    for kc in range(6):
                                        S.mm(bk.t[0:96, :], wuq.t[:, kc, h * 96:(h + 1) * 96], cqnT.t[:, kc, c * 512:(c + 1) * 512], kc == 0, kc == 5,
                                             [wuq.b, cqnT.b], [bk.b], sig=(kc == 5))
                                    rope_store(p2, bk, qmla[i].t[0:96, c * 512:(c + 1) * 512], 96, permM, cosM.t[0:96, c * 512:(c + 1) * 512],
                                               sinM.t[0:96, c * 512:(c + 1) * 512], [cosM.b, sinM.b], [qmla[i].b], tmps)
                                S.v("pool", "tensor_copy", [kropeT.b], [kaug[i].b], out=kaug[i].t[64:96, :], in_=kropeT.t[64:96, :])
                            for pr in range(2):
                                i0, i1 = 2 * pr, 2 * pr + 1
                                h0 = 4 * hf + i0
                                for c in range(NCH):
                                    bk = rotA.next()
                                    for kc in range(2):
                                        S.mm(bk.t[:, :], wuk.t[:, kc, h0 * 64:(h0 + 2) * 64], cnT.t[:, kc, c * 512:(c + 1) * 512], kc == 0, kc == 1,
                                             [wuk.b, cnT.b], [bk.b], sig=(kc == 1))
                                    S.v("dve", "tensor_copy", [bk.b], [kaug[i0].b], out=kaug[i0].t[0:64, c * 512:(c + 1) * 512], in_=bk.t[0:64, :])
                                    S.act(kaug[i1].t[0:64, c * 512:(c + 1) * 512], bk.t[64:128, :], AF.Copy, [bk.b], [kaug[i1].b])
                            for t in range(NT):
                                bk = rotA.next()
                                for kc in range(2):
                                    S.mm(bk.t[:, 0:256], cnT.t[:, kc, t * 128:(t + 1) * 128], wuv.t[:, kc, hf * 256:(hf + 1) * 256], kc == 0, kc == 1,
                                         [wuv.b, cnT.b], [bk.b], sig=(kc == 1))
                                if t % 2 == 0:
                                    S.act(V1m.t[:, t, :, 0:64], bk.t[:, 0:256].rearrange("p (g d) -> p g d", g=4), AF.Copy, [bk.b], [V1m.b])
                                else:
                                    S.v("dve", "tensor_copy", [bk.b], [V1m.b], out=V1m.t[:, t, :, 0:64], in_=bk.t[:, 0:256].rearrange("p (g d) -> p g d", g=4))
                            if s == 0 and hf == 0:
                                dump("qmla0", qmla[0].t[:], [qmla[0].b])
                                dump("kaug0", kaug[0].t[:], [kaug[0].b])
                            for c in range(NCH):
                                och = ochunks.next()

                                def mla_epi(acc, i, och=och):
                                    rz = rzs.next()
                                    accv = acc.t[:, 0:260].rearrange("p (b w) -> p b w", b=4)
                                    S.v("dve", "tensor_scalar", [acc.b], [rz.b], out=rz.t[:, 0:4].rearrange("p (b o) -> p b o", o=1),
                                        in0=accv[:, :, 64:65], scalar1=1e-30, scalar2=None, op0=ALU.max)
                                    S.v("dve", "reciprocal", [rz.b], [rz.b], out=rz.t[:, 0:4], in_=rz.t[:, 0:4])
                                    S.v("dve", "tensor_tensor", [acc.b, rz.b], [och.b], out=och.t[:, :, i * 64:(i + 1) * 64], in0=accv[:, :, 0:64],
                                        in1=rz.t[:, 0:4].rearrange("p (b o) -> p b o", o=1).broadcast_to([128, 4, 64]), op=ALU.mult)

                                jobs = []
                                for i in range(4):
                                    acc = rotB.next()
                                    tiles = []
                                    for t in range(4 * c + 4):
                                        if t < 4 * c:
                                            tiles.append((t, 0, 4, []))
                                        else:
                                            tiles.append((t, t - 4 * c, 4, [(0, 128, tri.t[:], tri.b)]))
                                    attn(c, tiles,
                                         lambda t, i=i: (kaug[i].t[0:96, t * 128:(t + 1) * 128], [kaug[i].b]),
                                         lambda c0, n, i=i: (qmla[i].t[0:96, c0:c0 + n], [qmla[i].b]),
                                         lambda t, i=i: (V1m.t[:, t, i, 0:65], [V1m.b]), 65, 96 ** -0.5, 128, acc, ptrot, jobs,
                                         epi=(lambda acc=acc, i=i: mla_epi(acc, i)))
                                run_jobs(jobs)
                                for j in range(2):
                                    tb = rotC.next()
                                    for b in range(4):
                                        S.tr(tb.t[:, b * 128:(b + 1) * 128], och.t[:, b, j * 128:(j + 1) * 128], identf.t[:],
                                             [och.b, identf.b], [tb.b], sig=(b == 3))
                                    if j == 0:
                                        S.act(omlaT.t[:, 2 * hf + j, c * 512:(c + 1) * 512], tb.t[:, :], AF.Copy, [tb.b], [omlaT.b])
                                    else:
                                        S.v("dve", "tensor_copy", [tb.b], [omlaT.b], out=omlaT.t[:, 2 * hf + j, c * 512:(c + 1) * 512], in_=tb.t[:, :])
                        S.barrier()
                    if s == 0:
                        dump("omlaT", omlaT.t[:].rearrange("p a b -> p (a b)"), [omlaT.b])
                        checkpoint("mla")
                    S.barrier()

                with contextlib.ExitStack() as mst:
                    xT = alloc(mst, "xT_bf", [128, 8, S_], BF16)
                    wm = Rot([alloc(mst, "wm%d" % i, [128, 24, 128], BF16) for i in range(3)])
                    gts = Rot([(alloc(mst, "g0_%d" % i, [128, 512], F32), alloc(mst, "g1_%d" % i, [128, 512], F32),
                                alloc(mst, "m0_%d" % i, [128, 512], F32), alloc(mst, "m1_%d" % i, [128, 512], F32)) for i in range(2)])
                    wa_v, wb_v = wview("nsa_w_o"), wview("mla_w_o")
                    def fetch_wm(j):
                        w = wm.next()
                        cs = slice(j * 128, (j + 1) * 128)
                        S.dma("pool", w.t[:, 0:4, :], wa_v[:, :, cs], [], [w.b], w.b)
                        S.dma("pool", w.t[:, 4:8, :], wb_v[:, :, cs], [], [w.b], w.b)
                        S.dma("pool", w.t[:, 8:16, :], w_in_v[:, :, O_MG + j * 128:O_MG + (j + 1) * 128], [], [w.b], w.b)
                        S.dma("pool", w.t[:, 16:24, :], w_in_v[:, :, O_MG + 1024 + j * 128:O_MG + 1024 + (j + 1) * 128], [], [w.b], w.b)
                        return w

                    wmq = [fetch_wm(0), fetch_wm(1)]
                    xsrc = dr["xT"][s].rearrange("(kc p) t -> p kc t", p=128)
                    for hf_ in range(2):
                        S.dma("pool", xT.t[:, :, hf_ * 1024:(hf_ + 1) * 1024], xsrc[:, :, hf_ * 1024:(hf_ + 1) * 1024], [], [xT.b], xT.b)
                    for j in range(8):
                        w = wmq.pop(0)
                        if j + 2 < 8:
                            wmq.append(fetch_wm(j + 2))
                        for c in range(NCH):
                            tc_ = slice(c * 512, (c + 1) * 512)
                            ya, yb, m0, m1 = rotA.next(), rotA.next(), rotB.next(), rotB.next()
                            for kc in range(4):
                                S.mm(ya.t[:, :], w.t[:, kc, :], onsaT.t[:, kc, tc_], kc == 0, kc == 3, [w.b, onsaT.b], [ya.b], sig=(kc == 3))
                            for kc in range(4):
                                S.mm(yb.t[:, :], w.t[:, 4 + kc, :], omlaT.t[:, kc, tc_], kc == 0, kc == 3, [w.b, omlaT.b], [yb.b], sig=(kc == 3))
                            for kc in range(8):
                                S.mm(m0.t[:, :], w.t[:, 8 + kc, :], xT.t[:, kc, tc_], kc == 0, kc == 7, [w.b, xT.b], [m0.b], sig=(kc == 7))
                            for kc in range(8):
                                S.mm(m1.t[:, :], w.t[:, 16 + kc, :], xT.t[:, kc, tc_], kc == 0, kc == 7, [w.b, xT.b], [m1.b], sig=(kc == 7))
                            g0, g1, t0, t1 = gts.next()
                            S.act(g0.t[:], m0.t[:, :], AF.Sigmoid, [m0.b], [g0.b])
                            S.act(g1.t[:], m1.t[:, :], AF.Sigmoid, [m1.b], [g1.b])
                            S.v("dve", "tensor_tensor", [ya.b, g0.b], [t0.b], out=t0.t[:], in0=ya.t[:, :], in1=g0.t[:], op=ALU.mult)
                            S.v("dve", "tensor_tensor", [yb.b, g1.b], [t1.b], out=t1.t[:], in0=yb.t[:, :], in1=g1.t[:], op=ALU.mult)
                            S.v("dve", "tensor_tensor", [t0.b, t1.b], [mergedT.b], out=mergedT.t[:, j, tc_], in0=t0.t[:], in1=t1.t[:], op=ALU.add)
                    S.barrier()
                S.barrier()
            if s == 0:
                dump("mergedT", mergedT.t[:].rearrange("p a b -> p (a b)"), [mergedT.b])
                checkpoint("merge")
                S.barrier()

            with contextlib.ExitStack() as fst:
                wout = alloc(fst, "wout", [128, 8, DM], BF16)
                wdn = alloc(fst, "wdn", [128, NJ, DM], BF16)
                for hh in range(2):
                    S.dma("pool", wout.t[:, :, hh * 512:(hh + 1) * 512], wview("w_out")[:, :, hh * 512:(hh + 1) * 512], [], [wout.b], wout.b)
                lnp = {}
                for nm in ("ln1_g", "ln1_b", "ln2_g", "ln2_b"):
                    lnp[nm] = alloc(fst, nm, [128, DM], F32)
                    S.dma("sp", lnp[nm].t[:], dr[nm].partition_broadcast(128)[:, 0, :], [], [lnp[nm].b], lnp[nm].b)
                cw = alloc(fst, "cw", [128, 3, NJ], F32)
                cb = alloc(fst, "cb", [128, NJ], F32)
                S.dma("sp", cw.t[:], dr["ffn_conv_w"], [], [cw.b], cw.b)
                S.dma("sp", cb.t[:], dr["ffn_conv_b"], [], [cb.b], cb.b)
                halo = alloc(fst, "halo", [128, NJ, 2], F32)
                S.v("pool", "memset", [], [halo.b], ap=halo.t[:], constant=0.0)
                x1 = alloc(fst, "x1", [128, 4, DM], F32)
                x1b = [Buf("x1_%d" % i) for i in range(4)]
                x1T = alloc(fst, "x1T", [128, 8, 512], BF16)
                hT = alloc(fst, "hT", [128, NJ, 512], BF16)
                xres = Rot([alloc(fst, "xres%d" % i, [128, DM], F32) for i in range(1)])
                rbuf = Rot([alloc(fst, "rbuf%d" % i, [128, DM], F32) for i in range(3)])
                lntmp = (alloc(fst, "lnstats", [128, 2, 6], F32), alloc(fst, "lnmv", [128, 2], F32), alloc(fst, "lnsc", [128, 4], F32))
                lntmp2 = (alloc(fst, "lnstats2", [128, 2, 6], F32), alloc(fst, "lnmv2", [128, 2], F32), alloc(fst, "lnsc2", [128, 4], F32))
                otile = Rot([alloc(fst, "otile%d" % i, [128, DM], F32) for i in range(1)])
                wgu = Rot([alloc(fst, "wgu%d" % i, [128, 2, 8, 128], BF16) for i in range(4)])
                a_sb = Rot([alloc(fst, "a_sb%d" % i, [128, 514], F32) for i in range(2)])
                ct = Rot([(alloc(fst, "ct0_%d" % i, [128, 512], F32), alloc(fst, "ct1_%d" % i, [128, 512], F32)) for i in range(2)])
                wg_v, wu_v = wview("ffn_w_gate"), wview("ffn_w_up")
                PF = 3
                wq = []
                rotU = Rot(banks[3:6])

                def fetch_w(j):
                    w = wgu.next()
                    cs = slice(j * 128, (j + 1) * 128)
                    S.dma("pool", w.t[:, 0, :, :], wg_v[:, :, cs], [], [w.b], w.b)
                    S.dma("pool", w.t[:, 1, :, :], wu_v[:, :, cs], [], [w.b], w.b)
                    return w

                def stage1_mm(blk, tt):
                    t = blk * 4 + tt
                    xr = xres.next()
                    S.dma("sp", xr.t[:], dr["x"][s, t * 128:(t + 1) * 128, :], [], [xr.b], xr.b)
                    bks = []
                    for hh in range(2):
                        bk = rotU.next()
                        for kc in range(8):
                            S.mm(bk.t[:, :], mergedT.t[:, kc, t * 128:(t + 1) * 128], wout.t[:, kc, hh * 512:(hh + 1) * 512], kc == 0, kc == 7,
                                 [mergedT.b, wout.b], [bk.b], sig=(kc == 7))
                        bks.append(bk)
                    return xr, bks

                def stage1_res(xr, bks):
                    r = rbuf.next()
                    for hh in range(2):
                        bk = bks[hh]
                        S.v("dve", "scalar_tensor_tensor", [xr.b, bk.b], [r.b], out=r.t[:, hh * 512:(hh + 1) * 512], in0=xr.t[:, hh * 512:(hh + 1) * 512],
                            scalar=ALPHA, in1=bk.t[:, :], op0=ALU.mult, op1=ALU.add)
                    return r

                def stage1_tr(tt):
                    for hh in range(2):
                        tb = rotC.next()
                        for k_ in range(4):
                            S.tr(tb.t[:, k_ * 128:(k_ + 1) * 128], x1.t[:, tt, (hh * 4 + k_) * 128:(hh * 4 + k_ + 1) * 128], identf.t[:],
                                 [x1b[tt], identf.b], [tb.b], sig=(k_ == 3))
                        if hh == 0:
                            S.act(x1T.t[:, 0:4, tt * 128:(tt + 1) * 128], tb.t[:, :].rearrange("p (k n) -> p k n", k=4), AF.Copy, [tb.b], [x1T.b])
                        else:
                            S.v("dve", "tensor_copy", [tb.b], [x1T.b], out=x1T.t[:, 4:8, tt * 128:(tt + 1) * 128],
                                in_=tb.t[:, :].rearrange("p (k n) -> p k n", k=4))

                def stage2(blk):
                    for j in range(NJ):
                        nxt = blk * NJ + j + PF
                        if nxt < 4 * NJ:
                            wq.append(fetch_w(nxt % NJ))
                        w = wq.pop(0)
                        ba, bu = rotA.next(), rotU.next()
                        for kc in range(8):
                            S.mm(ba.t[:, :], w.t[:, 0, kc, :], x1T.t[:, kc, :], kc == 0, kc == 7, [w.b, x1T.b], [ba.b], sig=(kc == 7))
                        for kc in range(8):
                            S.mm(bu.t[:, :], w.t[:, 1, kc, :], x1T.t[:, kc, :], kc == 0, kc == 7, [w.b, x1T.b], [bu.b], sig=(kc == 7))
                        asb = a_sb.next()
                        S.act(asb.t[:, 0:2], halo.t[:, j, :], AF.Copy, [halo.b], [asb.b])
                        S.act(asb.t[:, 2:514], ba.t[:, :], AF.Copy, [ba.b], [asb.b])
                        S.act(halo.t[:, j, :], asb.t[:, 512:514], AF.Copy, [asb.b], [halo.b])
                        c0, c1 = ct.next()
                        S.act(c0.t[:], ba.t[:, :], AF.Identity, [ba.b, cw.b, cb.b], [c0.b], scale=cw.t[:, 2, j:j + 1], bias=cb.t[:, j:j + 1])
                        S.v("dve", "scalar_tensor_tensor", [asb.b, cw.b, c0.b], [c1.b], out=c1.t[:], in0=asb.t[:, 1:513], scalar=cw.t[:, 1, j:j + 1],
                            in1=c0.t[:], op0=ALU.mult, op1=ALU.add)
                        S.v("dve", "scalar_tensor_tensor", [asb.b, cw.b, c1.b], [c0.b], out=c0.t[:], in0=asb.t[:, 0:512], scalar=cw.t[:, 0, j:j + 1],
                            in1=c1.t[:], op0=ALU.mult, op1=ALU.add)
                        S.act(c1.t[:], c0.t[:], AF.Gelu_apprx_tanh, [c0.b], [c1.b])
                        S.v("dve", "tensor_tensor", [c1.b, bu.b], [hT.b], out=hT.t[:, j, :], in0=bu.t[:, :], in1=c1.t[:], op=ALU.mult)

                def stage3_mm(blk, tt):
                    bks = []
                    for hh in range(2):
                        bk = rotA.next()
                        for j in range(NJ):
                            S.mm(bk.t[:, :], hT.t[:, j, tt * 128:(tt + 1) * 128], wdn.t[:, j, hh * 512:(hh + 1) * 512], j == 0, j == NJ - 1,
                                 [hT.b, wdn.b], [bk.b], sig=(j == NJ - 1))
                        bks.append(bk)
                    return bks

                def stage3_res(tt, bks):
                    r = rbuf.next()
                    for hh in range(2):
                        bk = bks[hh]
                        S.v("dve", "scalar_tensor_tensor", [x1b[tt], bk.b], [r.b], out=r.t[:, hh * 512:(hh + 1) * 512], in0=x1.t[:, tt, hh * 512:(hh + 1) * 512],
                            scalar=ALPHA, in1=bk.t[:, :], op0=ALU.mult, op1=ALU.add)
                    return r

                def stage3_ln(blk, tt, r):
                    t = blk * 4 + tt
                    ot = otile.next()
                    layer_norm(r, lnp["ln2_g"], lnp["ln2_b"], ot.t[:], ot.b, lntmp2)
                    S.dma("sp", out_d[s, t * 128:(t + 1) * 128, :], ot.t[:], [ot.b], [], outq)

                for jj in range(PF):
                    wq.append(fetch_w(jj))
                for j0 in range(0, NJ, 6):
                    j1 = min(NJ, j0 + 6)
                    S.dma("pool", wdn.t[:, j0:j1, :], wview("ffn_w_down")[:, j0:j1, :], [], [wdn.b], wdn.b)
                for blk in range(5):
                    for tt in range(4):
                        if blk < 4:
                            xr_, bks_ = stage1_mm(blk, tt)
                        bks3 = stage3_mm(blk - 1, tt) if blk >= 1 else None
                        if blk < 4:
                            r1 = stage1_res(xr_, bks_)
                            layer_norm(r1, lnp["ln1_g"], lnp["ln1_b"], None, None, lntmp)
                        if blk >= 1:
                            r2 = stage3_res(tt, bks3)
                        if blk < 4:
                            ln_bias(r1, lnp["ln1_b"], x1.t[:, tt, :], x1b[tt])
                        if blk >= 1:
                            stage3_ln(blk - 1, tt, r2)
                        if blk < 4 and tt >= 1:
                            stage1_tr(tt - 1)
                    if blk < 4:
                        stage1_tr(3)
                    if s == 0 and blk == 0:
                        dump("x1", x1.t[:].rearrange("p a b -> p (a b)"), x1b)
                    if blk < 4:
                        stage2(blk)
                S.barrier()
            S.barrier()
        S.barrier()
    return nc


def _host_inputs(inputs):
    f = lambda a: np.ascontiguousarray(np.asarray(a, dtype=np.float32))
    shared = {}
    for k_, shp in _W_SHAPES.items():
        if k_ == "pe_kT":
            a = f(inputs["cmp_pe_k"])[0].T
        elif k_ == "pe_vT":
            a = f(inputs["cmp_pe_v"])[0].T
        elif k_ in ("cmp_k_b1", "cmp_v_b1"):
            a = f(inputs[k_])[0].reshape(2, 128).T
        elif k_ == "ffn_conv_w":
            a = f(inputs[k_])[0].reshape(3, NJ, 128).transpose(2, 0, 1)
        elif k_ == "ffn_conv_b":
            a = f(inputs[k_])[0].reshape(NJ, 128).T
        else:
            a = f(inputs[k_])[0]
        shared[k_] = np.ascontiguousarray(a.reshape(shp))
    shared.update(_consts())
    return shared


def kernel(**inputs):
    x = np.ascontiguousarray(np.asarray(inputs["x"], dtype=np.float32))
    shared = _host_inputs(inputs)
    nc = build(NSEQ)
    in_maps = []
    for c in range(NCORES):
        xs = x[c * NSEQ:(c + 1) * NSEQ]
        m = dict(shared)
        m["x"] = np.ascontiguousarray(xs)
        m["xT"] = np.ascontiguousarray(xs.transpose(0, 2, 1))
        in_maps.append(m)
    res = run_bass_kernel_spmd(nc, in_maps, core_ids=list(range(NCORES)))
    return np.concatenate([np.asarray(r["out"], dtype=np.float32) for r in res.results], axis=0)
```

```python
import contextlib
import numpy as np
import ml_dtypes
import concourse.bass as bass
import concourse.mybir as mybir
from concourse.bass_utils import run_bass_kernel_spmd

F32 = mybir.dt.float32
BF16 = mybir.dt.bfloat16
AF = mybir.ActivationFunctionType
ALU = mybir.AluOpType

NCORES = 8
NSEQ = 2
S_ = 2048
DM = 1024
NT = 16
NCH = 4
DFF = 2816
NJ = 22
THETA = 500000.0
ALPHA = 2 ** 0.25
NEG = -30000.0
IN_W = 4408
O_NQ, O_KC, O_VC, O_KSL, O_VSL, O_KW, O_VW, O_NG, O_CQ, O_CKV, O_KR, O_MG = (
    0, 512, 640, 768, 896, 1024, 1152, 1280, 1304, 2072, 2328, 2360)


class Buf:
    def __init__(self, name):
        self.name = name
        self.last_write = None
        self.readers = {}
        self.excl = False


class TT:
    def __init__(self, t, b):
        self.t = t
        self.b = b


class Sched:
    def __init__(self, nc, stack):
        self.nc = nc
        self.stack = stack
        self.eng = {"pe": nc.tensor, "act": nc.scalar, "dve": nc.vector, "pool": nc.gpsimd, "sp": nc.sync}
        self.sem, self.cnt, self.seen = {}, {}, {}
        for k in self.eng:
            self.sem[k] = stack.enter_context(nc.semaphore("s_" + k))
            self.cnt[k] = 0
            self.seen[k] = {}
        self.dsem = {}
        self.halt = False

    def _wait(self, e, sem, val):
        seen = self.seen[e]
        key = id(sem)
        if seen.get(key, 0) >= val:
            return
        self.eng[e].wait_ge(sem, val)
        seen[key] = val

    def _deps(self, e, reads, writes, skip_sem=None):
        for b in reads:
            lw = b.last_write
            if lw is not None and not (e == "pe" and lw[2] == "pe"):
                self._wait(e, lw[0], lw[1])
            if b.excl:
                for re_, ev in b.readers.items():
                    if re_ != e:
                        self._wait(e, ev[0], ev[1])
        for b in writes:
            lw = b.last_write
            if lw is not None and not (e == "pe" and lw[2] == "pe") and not (skip_sem is not None and lw[0] is skip_sem):
                self._wait(e, lw[0], lw[1])
            for re_, ev in b.readers.items():
                if re_ == "pe" and e == "pe":
                    continue
                self._wait(e, ev[0], ev[1])

    def _post(self, e, ins, reads, writes, sig):
        ev = (self.sem[e], self.cnt[e] + 1, e)
        if sig:
            ins.then_inc(self.sem[e], 1)
            self.cnt[e] += 1
        else:
            assert e == "pe"
        for b in reads:
            b.readers[e] = ev
        for b in writes:
            b.last_write = ev
            b.readers = {}

    def mm(self, out, lhsT, rhs, start, stop, R, W, sig=True, skip=False):
        if self.halt:
            return
        self._deps("pe", R, W)
        ins = self.nc.tensor.matmul(out, lhsT=lhsT, rhs=rhs, start=start, stop=stop, skip_group_check=skip)
        self._post("pe", ins, R, W, sig)

    def tr(self, out, in_, ident, R, W, sig=True):
        if self.halt:
            return
        self._deps("pe", R, W)
        ins = self.nc.tensor.transpose(out=out, in_=in_, identity=ident)
        self._post("pe", ins, R, W, sig)

    def act(self, out, in_, func, R, W, **kw):
        if self.halt:
            return
        self._deps("act", R, W)
        ins = self.nc.scalar.activation(out=out, in_=in_, func=func, **kw)
        self._post("act", ins, R, W, True)

    def v(self, e, meth, R, W, **kw):
        if self.halt:
            return
        if e == "dve" and meth == "tensor_copy":
            meth = "tensor_scalar"
            kw = dict(out=kw["out"], in0=kw["in_"], scalar1=1.0, scalar2=None, op0=ALU.mult)
        self._deps(e, R, W)
        ins = getattr(self.eng[e], meth)(**kw)
        self._post(e, ins, R, W, True)

    def dma(self, q, out, in_, R, W, key):
        if self.halt:
            return
        if key.name not in self.dsem:
            self.dsem[key.name] = [self.stack.enter_context(self.nc.semaphore("d_" + key.name)), 0]
        ds = self.dsem[key.name]
        self._deps(q, R, W, skip_sem=ds[0])
        ins = self.eng[q].dma_start(out=out, in_=in_)
        ds[1] += 16
        ins.then_inc(ds[0], 16)
        ev = (ds[0], ds[1], "dma")
        for b in R:
            b.readers["dma_" + key.name] = ev
        for b in W:
            b.last_write = ev
            b.readers = {}

    def barrier(self):
        if self.halt:
            return
        for e in self.eng:
            for o in self.eng:
                if o != e:
                    self._wait(e, self.sem[o], self.cnt[o])
            for name, ds in self.dsem.items():
                if ds[1] > 0:
                    self._wait(e, ds[0], ds[1])


class Rot:
    def __init__(self, items):
        self.items = items
        self.i = 0

    def next(self):
        it = self.items[self.i % len(self.items)]
        self.i += 1
        return it


def _consts():
    bf = ml_dtypes.bfloat16
    pos = np.arange(S_, dtype=np.float32)
    invN = (THETA ** (-np.arange(0, 16, 2, dtype=np.float32) / 16)).astype(np.float32)
    angN = pos[None, :] * invN[:, None]
    cosN = np.ones((128, S_), np.float32)
    sinN = np.zeros((128, S_), np.float32)
    permN = np.zeros((128, 128), np.float32)
    for p in range(128):
        d = p % 64
        if d < 16:
            cosN[p] = np.cos(angN[d % 8])
            sinN[p] = -np.sin(angN[d]) if d < 8 else np.sin(angN[d - 8])
            src = p + 8 if d < 8 else p - 8
            permN[src, p] = 1.0
    invM = (THETA ** (-np.arange(0, 32, 2, dtype=np.float32) / 32)).astype(np.float32)
    angM = pos[None, :] * invM[:, None]
    cosM = np.ones((96, S_), np.float32)
    sinM = np.zeros((96, S_), np.float32)
    permM = np.zeros((96, 96), np.float32)
    for i in range(32):
        cosM[64 + i] = np.cos(angM[i % 16])
        sinM[64 + i] = -np.sin(angM[i]) if i < 16 else np.sin(angM[i - 16])
        permM[64 + (i + 16) % 32, 64 + i] = 1.0
    angT = angM.T.reshape(NT, 128, 16).transpose(1, 0, 2)
    cos2T = np.concatenate([np.cos(angT), np.cos(angT)], axis=2).astype(np.float32)
    sinST = np.concatenate([-np.sin(angT), np.sin(angT)], axis=2).astype(np.float32)
    n = np.arange(127)
    cmpmask = np.where((16 * n[:, None] + 31) <= pos[None, :].astype(np.int64), 0.0, NEG).astype(np.float32)
    cmpmask = np.concatenate([cmpmask, np.full((1, S_), NEG, np.float32)], axis=0)
    k = np.arange(128)
    tri = np.where(k[:, None] <= k[None, :], 0.0, NEG).astype(np.float32)
    tri2 = np.where(k[:, None] > k[None, :], 0.0, NEG).astype(np.float32)
    erow = np.zeros((96, S_), np.float32)
    for j in range(32):
        erow[64 + j, j * 64:(j + 1) * 64] = 1.0
    cs = n * 16
    ce = cs + 31
    ss = np.arange(32) * 64
    ovl = ((cs[:, None] <= ss[None, :] + 63) & (ce[:, None] >= ss[None, :])).astype(np.float32)
    vca = np.zeros((128, 2, 97), np.float32)
    vca[:, :, 64] = 1.0
    vca[:127, :, 65:97] = ovl[:, None, :]
    p = np.arange(S_)
    cur = p // 64
    j = np.arange(32)
    valid = j[None, :] <= cur[:, None]
    forced = ((j[None, :] == 0) | (j[None, :] == cur[:, None]) | (j[None, :] == cur[:, None] - 1)) & valid
    bonus = np.where(forced, 100.0, np.where(valid, 0.0, -1.0)).astype(np.float32)
    bonus = bonus.reshape(NT, 128, 32).transpose(1, 0, 2)
    ident = np.eye(128, dtype=np.float32)
    return {
        "c_cosN": cosN, "c_sinN": sinN, "c_permN": permN.astype(bf),
        "c_cosM": cosM, "c_sinM": sinM, "c_permM": permM.astype(bf),
        "c_cos2T": np.ascontiguousarray(cos2T), "c_sinST": np.ascontiguousarray(sinST),
        "c_cmpmask": cmpmask.astype(bf), "c_tri": tri.astype(bf), "c_tri2": tri2.astype(bf),
        "c_erow": erow.astype(bf), "c_vca": vca.astype(bf), "c_bonus": np.ascontiguousarray(bonus),
        "c_identf": ident, "c_identb": ident.astype(bf),
    }


_CONST_SHAPES = {
    "c_cosN": ([128, S_], F32), "c_sinN": ([128, S_], F32), "c_permN": ([128, 128], BF16),
    "c_cosM": ([96, S_], F32), "c_sinM": ([96, S_], F32), "c_permM": ([96, 96], BF16),
    "c_cos2T": ([128, NT, 32], F32), "c_sinST": ([128, NT, 32], F32),
    "c_cmpmask": ([128, S_], BF16), "c_tri": ([128, 128], BF16), "c_tri2": ([128, 128], BF16),
    "c_erow": ([96, S_], BF16), "c_vca": ([128, 2, 97], BF16), "c_bonus": ([128, NT, 32], F32),
    "c_identf": ([128, 128], F32), "c_identb": ([128, 128], BF16),
}

_W_SHAPES = {
    "w_in": [DM, IN_W], "pe_kT": [64, 32], "pe_vT": [64, 32],
    "cmp_k_w1": [2048, 256], "cmp_k_b1": [128, 2], "cmp_k_w2": [256, 64],
    "cmp_v_w1": [2048, 256], "cmp_v_b1": [128, 2], "cmp_v_w2": [256, 64],
    "nsa_w_o": [512, DM], "mla_q_norm": [1, 768], "mla_w_uq": [768, 768], "mla_kv_norm": [1, 256],
    "mla_w_uk": [256, 512], "mla_w_uv": [256, 512], "mla_w_o": [512, DM], "w_out": [DM, DM],
    "ln1_g": [1, DM], "ln1_b": [1, DM], "ffn_w_gate": [DM, DFF], "ffn_w_up": [DM, DFF],
    "ffn_conv_w": [128, 3, NJ], "ffn_conv_b": [128, NJ], "ffn_w_down": [DFF, DM], "ln2_g": [1, DM], "ln2_b": [1, DM],
}


class _Stop(Exception):
    pass


def build(nseq=NSEQ, debug=None, stop=None):
    nc = bass.Bass("TRN2", target_bir_lowering=False)
    dr = {}
    dr["xT"] = nc.dram_tensor("xT", [nseq, DM, S_], F32, kind="ExternalInput").ap()
    dr["x"] = nc.dram_tensor("x", [nseq, S_, DM], F32, kind="ExternalInput").ap()
    for k_, shp in _W_SHAPES.items():
        dr[k_] = nc.dram_tensor(k_, shp, F32, kind="ExternalInput").ap()
    for k_, (shp, dt) in _CONST_SHAPES.items():
        dr[k_] = nc.dram_tensor(k_, shp, dt, kind="ExternalInput").ap()
    out_d = nc.dram_tensor("out", [nseq, S_, DM], F32, kind="ExternalOutput").ap()
    dbg_d = {}
    if debug:
        for name, shp in debug.items():
            dbg_d[name] = nc.dram_tensor("dbg_" + name, shp, F32, kind="ExternalOutput").ap()

    with contextlib.ExitStack() as top:
        S = Sched(nc, top)

        uid = [0]

        def alloc(st, name, shape, dt=F32):
            uid[0] += 1
            return TT(st.enter_context(nc.sbuf_tensor("sb%d_%s" % (uid[0], name), shape, dt)), Buf(name))

        banks = [TT(top.enter_context(nc.psum_tensor("bank%d" % i, [128, 512], F32)), Buf("bank%d" % i)) for i in range(8)]
        for bk_ in banks:
            bk_.b.excl = True
        rotA = Rot(banks[0:3])
        rotB = Rot(banks[3:5] + [banks[6]])
        rotC = Rot(banks[5:8])

        def cload(st, name, q="sp"):
            shp, dt = _CONST_SHAPES[name]
            t = alloc(st, name, shp, dt)
            S.dma(q, t.t[:], dr[name], [], [t.b], t.b)
            return t

        identf = cload(top, "c_identf")
        identb = cload(top, "c_identb")
        permN = cload(top, "c_permN")
        permM = cload(top, "c_permM")
        tri = cload(top, "c_tri")
        tri2 = cload(top, "c_tri2")
        dbg_out = Buf("dbgout")

        def dump(name, src_ap, R):
            if name in dbg_d:
                S.dma("pool", dbg_d[name], src_ap, R, [], dbg_out)

        def wview(name, p=128):
            return dr[name].rearrange("(kc p) c -> p kc c", p=p)

        w_in_v = wview("w_in")
        outq = Buf("outq")

        def rope_store(st_tmp, bank, dest_ap, nrows, perm, cos_ap, sin_ap, Rtab, Wdest, tmps):
            qraw, t1, t2 = tmps.next()
            S.act(qraw.t[0:nrows, :], bank.t[0:nrows, :], AF.Copy, [bank.b], [qraw.b])
            rb = rotB.next()
            S.mm(rb.t[0:nrows, :], perm.t[0:nrows, 0:nrows], qraw.t[0:nrows, :], True, True, [perm.b, qraw.b], [rb.b])
            S.v("dve", "tensor_tensor", [rb.b] + Rtab, [t1.b], out=t1.t[0:nrows, :], in0=rb.t[0:nrows, :], in1=sin_ap, op=ALU.mult)
            S.v("dve", "tensor_tensor", [bank.b] + Rtab, [t2.b], out=t2.t[0:nrows, :], in0=bank.t[0:nrows, :], in1=cos_ap, op=ALU.mult)
            S.v("pool", "tensor_tensor", [t1.b, t2.b], Wdest, out=dest_ap, in0=t1.t[0:nrows, :], in1=t2.t[0:nrows, :], op=ALU.add)

        rotS = Rot(banks[0:3] + [banks[5]])

        def attn(c, tiles, kT_fn, q_fn, v_fn, W, scale, nk, acc, ptrot, jobs, epi=None):
            lastt = {}
            for (t, b0, b1, masks) in tiles:
                for b in range(b0, b1):
                    lastt[b] = t
            for idx, (t, b0, b1, masks) in enumerate(tiles):
                jobs.append(dict(c=c, t=t, b0=b0, b1=b1, masks=masks, k=kT_fn(t), q=q_fn(c * 512 + b0 * 128, (b1 - b0) * 128),
                                 v=v_fn(t), W=W, scale=scale, nk=nk, acc=acc, first=(idx == 0), lastt=lastt, ptrot=ptrot,
                                 epi=(epi if idx == len(tiles) - 1 else None)))

        def job_score(j):
            n = (j["b1"] - j["b0"]) * 128
            nk = j["nk"]
            sb = rotS.next()
            j["sb"] = sb
            kap, kb = j["k"]
            qap, qb = j["q"]
            masks = j["masks"]
            S.mm(sb.t[0:nk, 0:n], kap, qap, True, not masks, kb + qb, [sb.b], sig=not masks)
            for i, (c0, ncol, map_, mb) in enumerate(masks):
                last = i == len(masks) - 1
                S.mm(sb.t[0:nk, c0:c0 + ncol], identb.t[0:nk, 0:nk], map_, False, last, [identb.b, mb], [sb.b], sig=last)

        def job_rest(j):
            n = (j["b1"] - j["b0"]) * 128
            nk, W, acc, sb, b0, b1 = j["nk"], j["W"], j["acc"], j["sb"], j["b0"], j["b1"]
            pt = j["ptrot"].next()
            S.act(pt.t[0:nk, 0:n], sb.t[0:nk, 0:n], AF.Exp, [sb.b], [pt.b], scale=j["scale"])
            vap, vb = j["v"]
            for b in range(b0, b1):
                S.mm(acc.t[:, b * W:(b + 1) * W], pt.t[0:nk, (b - b0) * 128:(b - b0 + 1) * 128], vap,
                     j["first"] and b == b0, j["lastt"][b] == j["t"], [pt.b] + vb, [acc.b], sig=(b == b1 - 1), skip=True)
            if j["epi"] is not None:
                j["epi"]()

        def run_jobs(jobs, L=3):
            n = len(jobs)
            si = 0
            ahead = 0
            for idx, j in enumerate(jobs):
                if "hook" in j:
                    j["hook"]()
                    if si <= idx:
                        si = idx + 1
                    continue
                if si < idx:
                    si = idx
                while si < n and ahead <= L:
                    js = jobs[si]
                    if "hook" in js:
                        if js.get("block"):
                            break
                        si += 1
                        continue
                    job_score(js)
                    ahead += 1
                    si += 1
                job_rest(j)
                ahead -= 1

        def layer_norm(r, gt, bt, out_ap, outbuf, tmp):
            stats, mv, sc = tmp
            for hh in range(2):
                S.v("dve", "bn_stats", [r.b], [stats.b], out=stats.t[:, hh, :], in_=r.t[:, hh * 512:(hh + 1) * 512])
            S.v("dve", "bn_aggr", [stats.b], [mv.b], out=mv.t[:, 0:2], in_=stats.t[:].rearrange("p a b -> p (a b)"))
            S.v("dve", "tensor_scalar", [mv.b], [sc.b], out=sc.t[:, 0:1], in0=mv.t[:, 1:2], scalar1=1e-5, scalar2=None, op0=ALU.add)
            S.act(sc.t[:, 1:2], sc.t[:, 0:1], AF.Ln, [sc.b], [sc.b])
            S.act(sc.t[:, 2:3], sc.t[:, 1:2], AF.Exp, [sc.b], [sc.b], scale=-0.5)
            S.v("dve", "tensor_scalar", [r.b, mv.b, sc.b], [r.b], out=r.t[:], in0=r.t[:], scalar1=mv.t[:, 0:1], scalar2=sc.t[:, 2:3],
                op0=ALU.subtract, op1=ALU.mult)
            S.v("dve", "tensor_tensor", [r.b, gt.b], [r.b], out=r.t[:], in0=r.t[:], in1=gt.t[:], op=ALU.mult)
            if out_ap is not None:
                ln_bias(r, bt, out_ap, outbuf)

        def ln_bias(r, bt, out_ap, outbuf):
            S.v("pool", "tensor_tensor", [r.b, bt.b], [outbuf], out=out_ap, in0=r.t[:], in1=bt.t[:], op=ALU.add)

        def checkpoint(name):
            if stop == name:
                S.barrier()
                S.halt = True

        checkpoint("init")
        for s in range(nseq):
          with contextlib.ExitStack() as seqst:
            mergedT = alloc(seqst, "mergedT", [128, 8, S_], BF16)
            with contextlib.ExitStack() as mixst:
                onsaT = alloc(mixst, "onsaT", [128, 4, S_], BF16)
                omlaT = alloc(mixst, "omlaT", [128, 4, S_], BF16)

                def load_xT(st):
                    xT = alloc(st, "xT_bf", [128, 8, S_], BF16)
                    xsrc = dr["xT"][s].rearrange("(kc p) t -> p kc t", p=128)
                    for hf_ in range(2):
                        S.dma("pool", xT.t[:, :, hf_ * 1024:(hf_ + 1) * 1024], xsrc[:, :, hf_ * 1024:(hf_ + 1) * 1024], [], [xT.b], xT.b)
                    return xT

                with contextlib.ExitStack() as nsast:
                    xT = load_xT(nsast)
                    cmpmask = cload(nsast, "c_cmpmask")
                    bonus = cload(nsast, "c_bonus")
                    qpair = [alloc(nsast, "qpair%d" % i, [128, S_], BF16) for i in range(4)]
                    kcpair = alloc(nsast, "kcpair", [128, S_], BF16)
                    vcpair = alloc(nsast, "vcpair", [128, S_], BF16)
                    kwpair = alloc(nsast, "kwpair", [128, S_], BF16)
                    kslpair = alloc(nsast, "kslpair", [128, S_], BF16)
                    ksel = [alloc(nsast, "ksel%d" % g, [96, S_], BF16) for g in range(2)]
                    V1sel = alloc(nsast, "V1sel", [128, NT, 2, 66], BF16)
                    V1win = alloc(nsast, "V1win", [128, NT, 2, 66], BF16)
                    gates = alloc(nsast, "gates", [128, NT, 24], F32)
                    kcT = alloc(nsast, "kcT", [128, 128], BF16)
                    vca = alloc(nsast, "vca", [128, 2, 97], BF16)
                    S.dma("sp", vca.t[:], dr["c_vca"], [], [vca.b], vca.b)
                    for g in range(2):
                        S.dma("sp", ksel[g].t[64:96, :], dr["c_erow"][64:96, :], [], [ksel[g].b], ksel[g].b)
                    S.v("pool", "memset", [], [V1sel.b], ap=V1sel.t[:, :, :, 64:65], constant=1.0)
                    S.v("pool", "memset", [], [V1win.b], ap=V1win.t[:, :, :, 64:65], constant=1.0)

                    wtm = alloc(nsast, "wtm", [128, 8, 280], BF16)
                    egt = alloc(nsast, "egt", [128, 24], F32)
                    with contextlib.ExitStack() as pst:
                        cosN = cload(pst, "c_cosN")
                        sinN = cload(pst, "c_sinN")
                        wbufs = Rot([alloc(pst, "wbuf%d" % i, [128, 8, 128], BF16) for i in range(8)])
                        tmps = Rot([(alloc(pst, "qraw%d" % i, [128, 512], BF16), alloc(pst, "rt1_%d" % i, [128, 512], F32),
                                     alloc(pst, "rt2_%d" % i, [128, 512], F32)) for i in range(2)])
                        groups = []
                        for p_ in range(4):
                            groups.append(([(p_ * 64, 64), ((4 + p_) * 64, 64)], qpair[p_], True))
                        groups.append(([(O_KC, 128)], kcpair, True))
                        groups.append(([(O_VC, 128)], vcpair, False))
                        groups.append(([(O_KSL, 128)], kslpair, True))
                        groups.append(([(O_KW, 128)], kwpair, True))
                        gw = []
                        for (cols, dest, rope) in groups:
                            wb = wbufs.next()
                            d0 = 0
                            for (c0, n) in cols:
                                S.dma("pool", wb.t[:, :, d0:d0 + n], w_in_v[:, :, c0:c0 + n], [], [wb.b], wb.b)
                                d0 += n
                            gw.append(wb)
                        for (c0, n, d0) in ((O_VSL, 128, 0), (O_VW, 128, 128), (O_NG, 24, 256)):
                            S.dma("pool", wtm.t[:, :, d0:d0 + n], w_in_v[:, :, c0:c0 + n], [], [wtm.b], wtm.b)
                        for gi, (cols, dest, rope) in enumerate(groups):
                            wb = gw[gi]
                            for c in range(NCH):
                                bk = rotA.next()
                                for kc in range(8):
                                    S.mm(bk.t[:, :], wb.t[:, kc, :], xT.t[:, kc, c * 512:(c + 1) * 512], kc == 0, kc == 7,
                                         [wb.b, xT.b], [bk.b], sig=(kc == 7))
                                dst = dest.t[:, c * 512:(c + 1) * 512]
                                if rope:
                                    rope_store(pst, bk, dst, 128, permN, cosN.t[:, c * 512:(c + 1) * 512],
                                               sinN.t[:, c * 512:(c + 1) * 512], [cosN.b, sinN.b], [dest.b], tmps)
                                else:
                                    S.act(dst, bk.t[:, :], AF.Copy, [bk.b], [dest.b])
                        checkpoint("fm")
                        S.v("pool", "tensor_copy", [kslpair.b], [ksel[0].b], out=ksel[0].t[0:64, :], in_=kslpair.t[0:64, :])
                        S.act(ksel[1].t[0:64, :], kslpair.t[64:128, :], AF.Copy, [kslpair.b], [ksel[1].b])
                        checkpoint("ksel")
                        S.barrier()
                    if s == 0:
                        checkpoint("pre")
                        dump("qpair0", qpair[0].t[:], [qpair[0].b])
                        dump("kcpair", kcpair.t[:], [kcpair.b])
                        dump("gates", gates.t[:].rearrange("p a b -> p (a b)"), [gates.b])
                        dump("V1sel", V1sel.t[:].rearrange("p a b c -> p (a b c)"), [V1sel.b])
                        checkpoint("proj")

                    with contextlib.ExitStack() as cst:
                        w1s = {"k": alloc(cst, "w1k", [128, 32, 256], BF16), "v": alloc(cst, "w1v", [128, 32, 256], BF16)}
                        for typ in ("k", "v"):
                            w1v = dr["cmp_%s_w1" % typ].rearrange("(l d) h -> d l h", d=64)
                            for g in range(2):
                                S.dma("pool", w1s[typ].t[g * 64:(g + 1) * 64, :, :], w1v, [], [w1s[typ].b], w1s[typ].b)
                        for t in range(NT):
                            bk = rotA.next()
                            for kc in range(8):
                                S.mm(bk.t[:, 0:280], xT.t[:, kc, t * 128:(t + 1) * 128], wtm.t[:, kc, :], kc == 0, kc == 7,
                                     [wtm.b, xT.b], [bk.b], sig=(kc == 7))
                            checkpoint("tm_mm")
                            S.act(V1sel.t[:, t, :, 0:64], bk.t[:, 0:128].rearrange("p (g d) -> p g d", g=2), AF.Copy, [bk.b], [V1sel.b])
                            checkpoint("tm_v1")
                            S.v("dve", "tensor_copy", [bk.b], [V1win.b], out=V1win.t[:, t, :, 0:64],
                                in_=bk.t[:, 128:256].rearrange("p (g d) -> p g d", g=2))
                            checkpoint("tm_v2")
                            S.act(egt.t[:], bk.t[:, 256:280], AF.Exp, [bk.b], [egt.b], scale=-1.0)
                            checkpoint("tm_e")
                            S.v("dve", "tensor_scalar", [egt.b], [egt.b], out=egt.t[:], in0=egt.t[:], scalar1=1.0, scalar2=None, op0=ALU.add)
                            S.v("dve", "reciprocal", [egt.b], [gates.b], out=gates.t[:, t, :], in_=egt.t[:])
                        w2 = alloc(cst, "w2", [128, 2, 64], BF16)
                        b1 = alloc(cst, "b1", [128, 2], F32)
                        peT = alloc(cst, "peT", [64, 32], BF16)
                        biasc = alloc(cst, "biasc", [128, 2], F32)
                        H1g = alloc(cst, "H1g", [128, 2, 2, 128], BF16)
                        for typ in ("k", "v"):
                            src = kcpair if typ == "k" else vcpair
                            w1 = w1s[typ]
                            S.dma("pool", w2.t[:], dr["cmp_%s_w2" % typ].rearrange("(hh p) d -> p hh d", p=128), [], [w2.b], w2.b)
                            S.dma("sp", b1.t[:], dr["cmp_%s_b1" % typ], [], [b1.b], b1.b)
                            S.dma("pool", peT.t[:], dr["pe_%sT" % typ], [], [peT.b], peT.b)
                            bb = rotC.next()
                            for hh in range(2):
                                for l in range(32):
                                    S.mm(bb.t[:, hh:hh + 1], w1.t[0:64, l, hh * 128:(hh + 1) * 128], peT.t[0:64, l:l + 1],
                                         l == 0, l == 31, [w1.b, peT.b], [bb.b], sig=(l == 31))
                            S.v("dve", "tensor_tensor", [bb.b, b1.b], [biasc.b], out=biasc.t[:], in0=bb.t[:, 0:2], in1=b1.t[:], op=ALU.add)
                            for g in range(2):
                                for hh in range(2):
                                    hb = rotA.next()
                                    for l in range(32):
                                        S.mm(hb.t[:, 0:127], w1.t[g * 64:(g + 1) * 64, l, hh * 128:(hh + 1) * 128],
                                             src.t[g * 64:(g + 1) * 64, l:l + 2017:16], l == 0, l == 31, [w1.b, src.b], [hb.b], sig=(l == 31))
                                    S.act(H1g.t[:, g, hh, 0:127], hb.t[:, 0:127], AF.Gelu_apprx_tanh, [hb.b, biasc.b], [H1g.b],
                                          bias=biasc.t[:, hh:hh + 1])
                            ob = rotB.next()
                            if typ == "k":
                                for g in range(2):
                                    for hh in range(2):
                                        S.mm(ob.t[g * 64:(g + 1) * 64, 0:127], w2.t[:, hh, :], H1g.t[:, g, hh, 0:127], hh == 0, hh == 1,
                                             [w2.b, H1g.b], [ob.b], sig=(hh == 1))
                                S.act(kcT.t[:, 0:127], ob.t[:, 0:127], AF.Copy, [ob.b], [kcT.b])
                            else:
                                for g in range(2):
                                    for hh in range(2):
                                        S.mm(ob.t[0:127, g * 64:(g + 1) * 64], H1g.t[:, g, hh, 0:127], w2.t[:, hh, :], hh == 0, hh == 1,
                                             [w2.b, H1g.b], [ob.b], sig=(hh == 1))
                                S.act(vca.t[0:127, :, 0:64], ob.t[0:127, 0:128].rearrange("p (g d) -> p g d", g=2), AF.Copy, [ob.b], [vca.b])
                        S.barrier()
                    if s == 0:
                        dump("kcT", kcT.t[:], [kcT.b])
                        dump("vca", vca.t[:].rearrange("p a b -> p (a b)"), [vca.b])
                        checkpoint("cmp")

                    with contextlib.ExitStack() as ast:
                        qaug = [alloc(ast, "qaug%d" % i, [96, S_], BF16) for i in range(4)]
                        ptrot = Rot([alloc(ast, "pt%d" % i, [128, 512], BF16) for i in range(3)])
                        ochunks = Rot([alloc(ast, "och%d" % i, [128, 4, 256], F32) for i in range(2)])
                        otmps = Rot([alloc(ast, "otmp%d" % i, [128, 4, 64], F32) for i in range(2)])
                        impacc = alloc(ast, "impacc", [128, 4, 32], F32)
                        itmps = Rot([alloc(ast, "itmp%d" % i, [128, 4, 32], F32) for i in range(2)])
                        rzs = Rot([alloc(ast, "rz%d" % i, [128, 8], F32) for i in range(3)])
                        top8 = alloc(ast, "top8", [128, 4, 8], F32)
                        negm = alloc(ast, "negm", [128, 4, 96], F32)
                        S.v("pool", "memset", [], [negm.b], ap=negm.t[:], constant=0.0)
                        def nsa_epi(acc, W, i, br, first, with_imp, g, c, och):
                            rz = rzs.next()
                            accv = acc.t[:, 0:4 * W].rearrange("p (b w) -> p b w", b=4)
                            S.v("dve", "tensor_scalar", [acc.b], [rz.b], out=rz.t[:, 0:4].rearrange("p (b o) -> p b o", o=1),
                                in0=accv[:, :, 64:65], scalar1=1e-30, scalar2=None, op0=ALU.max)
                            S.v("dve", "reciprocal", [rz.b], [rz.b], out=rz.t[:, 0:4], in_=rz.t[:, 0:4])
                            h = 4 * g + i
                            S.v("dve", "tensor_tensor", [rz.b, gates.b], [rz.b], out=rz.t[:, 4:8].rearrange("p (b o) -> p b o", o=1),
                                in0=rz.t[:, 0:4].rearrange("p (b o) -> p b o", o=1),
                                in1=gates.t[:, 4 * c:4 * c + 4, h * 3 + br:h * 3 + br + 1], op=ALU.mult)
                            gs_bc = rz.t[:, 4:8].rearrange("p (b o) -> p b o", o=1).broadcast_to([128, 4, 64])
                            if first:
                                S.v("dve", "tensor_tensor", [acc.b, rz.b], [och.b], out=och.t[:, :, i * 64:(i + 1) * 64],
                                    in0=accv[:, :, 0:64], in1=gs_bc, op=ALU.mult)
                            else:
                                ot = otmps.next()
                                S.v("dve", "tensor_tensor", [acc.b, rz.b], [ot.b], out=ot.t[:], in0=accv[:, :, 0:64], in1=gs_bc, op=ALU.mult)
                                S.v("pool", "tensor_tensor", [ot.b, och.b], [och.b], out=och.t[:, :, i * 64:(i + 1) * 64],
                                    in0=och.t[:, :, i * 64:(i + 1) * 64], in1=ot.t[:], op=ALU.add)
                            if with_imp:
                                it = itmps.next()
                                rz_bc = rz.t[:, 0:4].rearrange("p (b o) -> p b o", o=1).broadcast_to([128, 4, 32])
                                S.v("dve", "tensor_tensor", [acc.b, rz.b], [it.b], out=it.t[:], in0=accv[:, :, 65:97], in1=rz_bc, op=ALU.mult)
                                S.v("pool", "tensor_tensor", [it.b, impacc.b], [impacc.b], out=impacc.t[:], in0=impacc.t[:], in1=it.t[:], op=ALU.add)

                        def build_cmp(g, c, och, jobs):
                            pb = g * 64
                            for i in range(4):
                                acc = rotB.next()
                                attn(c, [(0, 0, 4, [(0, 512, cmpmask.t[0:127, c * 512:(c + 1) * 512], cmpmask.b)])],
                                     lambda t: (kcT.t[pb:pb + 64, 0:127], [kcT.b]),
                                     lambda c0, n, i=i: (qpair[i].t[pb:pb + 64, c0:c0 + n], [qpair[i].b]),
                                     lambda t: (vca.t[0:127, g, :], [vca.b]), 97, 0.125, 127, acc, ptrot, jobs,
                                     epi=(lambda acc=acc, i=i: nsa_epi(acc, 97, i, 0, True, True, g, c, och)))

                        def build_win(g, c, och, jobs):
                            pb = g * 64
                            for i in range(4):
                                acc = rotB.next()
                                tiles = []
                                for t in range(max(0, 4 * c - 2), 4 * c + 4):
                                    b0 = max(t, 4 * c) - 4 * c
                                    b1 = min(t + 2, 4 * c + 3) - 4 * c + 1
                                    masks = []
                                    if t >= 4 * c:
                                        masks.append((0, 128, tri.t[:], tri.b))
                                    if t + 2 <= 4 * c + 3:
                                        masks.append(((t + 2 - 4 * c - b0) * 128, 128, tri2.t[:], tri2.b))
                                    tiles.append((t, b0, b1, masks))
                                attn(c, tiles,
                                     lambda t: (kwpair.t[pb:pb + 64, t * 128:(t + 1) * 128], [kwpair.b]),
                                     lambda c0, n, i=i: (qpair[i].t[pb:pb + 64, c0:c0 + n], [qpair[i].b]),
                                     lambda t: (V1win.t[:, t, g, 0:65], [V1win.b]), 65, 0.125, 128, acc, ptrot, jobs,
                                     epi=(lambda acc=acc, i=i: nsa_epi(acc, 65, i, 2, False, False, g, c, och)))

                        def build_sel(g, c, och, jobs):
                            for i in range(4):
                                acc = rotB.next()
                                tiles = []
                                for t in range(4 * c + 4):
                                    if t < 4 * c:
                                        tiles.append((t, 0, 4, []))
                                    else:
                                        tiles.append((t, t - 4 * c, 4, [(0, 128, tri.t[:], tri.b)]))
                                attn(c, tiles,
                                     lambda t: (ksel[g].t[0:96, t * 128:(t + 1) * 128], [ksel[g].b]),
                                     lambda c0, n, i=i: (qaug[i].t[0:96, c0:c0 + n], [qaug[i].b]),
                                     lambda t: (V1sel.t[:, t, g, 0:65], [V1sel.b]), 65, 0.125, 128, acc, ptrot, jobs,
                                     epi=(lambda acc=acc, i=i: nsa_epi(acc, 65, i, 1, False, False, g, c, och)))

                        def hook_init(c):
                            S.v("pool", "tensor_copy", [bonus.b], [impacc.b], out=impacc.t[:], in_=bonus.t[:, 4 * c:4 * c + 4, :])

                        def hook_topk():
                            for b in range(4):
                                S.v("dve", "max", [impacc.b], [top8.b], out=top8.t[:, b, :], in_=impacc.t[:, b, :])
                            S.v("dve", "tensor_tensor", [impacc.b, top8.b], [negm.b], out=negm.t[:, :, 64:96], in0=impacc.t[:],
                                in1=top8.t[:, :, 7:8].broadcast_to([128, 4, 32]), op=ALU.is_lt)
                            S.v("dve", "tensor_scalar", [negm.b], [negm.b], out=negm.t[:, :, 64:96], in0=negm.t[:, :, 64:96],
                                scalar1=NEG, scalar2=None, op0=ALU.mult)

                        def hook_neg(c):
                            nb = rotC.next()
                            for b in range(4):
                                S.tr(nb.t[0:96, b * 128:(b + 1) * 128], negm.t[:, b, :], identf.t[:], [negm.b, identf.b], [nb.b], sig=(b == 3))
                            for i in range(4):
                                if i % 2 == 0:
                                    S.act(qaug[i].t[64:96, c * 512:(c + 1) * 512], nb.t[64:96, :], AF.Copy, [nb.b], [qaug[i].b])
                                else:
                                    S.v("dve", "tensor_copy", [nb.b], [qaug[i].b], out=qaug[i].t[64:96, c * 512:(c + 1) * 512], in_=nb.t[64:96, :])

                        def hook_och(g, c, och):
                            for j in range(2):
                                tb = rotC.next()
                                for b in range(4):
                                    S.tr(tb.t[:, b * 128:(b + 1) * 128], och.t[:, b, j * 128:(j + 1) * 128], identf.t[:],
                                         [och.b, identf.b], [tb.b], sig=(b == 3))
                                if j == 0:
                                    S.act(onsaT.t[:, 2 * g + j, c * 512:(c + 1) * 512], tb.t[:, :], AF.Copy, [tb.b], [onsaT.b])
                                else:
                                    S.v("dve", "tensor_copy", [tb.b], [onsaT.b], out=onsaT.t[:, 2 * g + j, c * 512:(c + 1) * 512], in_=tb.t[:, :])

                        for g in range(2):
                            for i in range(4):
                                if g == 0:
                                    S.v("pool", "tensor_copy", [qpair[i].b], [qaug[i].b], out=qaug[i].t[0:64, :], in_=qpair[i].t[0:64, :])
                                else:
                                    S.act(qaug[i].t[0:64, :], qpair[i].t[64:128, :], AF.Copy, [qpair[i].b], [qaug[i].b])
                            jobs = []
                            och = {0: ochunks.next()}
                            jobs.append(dict(hook=(lambda: hook_init(0))))
                            build_cmp(g, 0, och[0], jobs)
                            jobs.append(dict(hook=hook_topk))
                            build_win(g, 0, och[0], jobs)
                            jobs.append(dict(hook=(lambda: hook_neg(0)), block=True))
                            for c in range(NCH):
                                if c + 1 < NCH:
                                    och[c + 1] = ochunks.next()
                                    jobs.append(dict(hook=(lambda c=c: hook_init(c + 1))))
                                    build_cmp(g, c + 1, och[c + 1], jobs)
                                    jobs.append(dict(hook=hook_topk))
                                build_sel(g, c, och[c], jobs)
                                if c + 1 < NCH:
                                    build_win(g, c + 1, och[c + 1], jobs)
                                    jobs.append(dict(hook=(lambda c=c: hook_neg(c + 1)), block=True))
                                jobs.append(dict(hook=(lambda c=c, g=g, o=och[c]: hook_och(g, c, o))))
                            run_jobs(jobs)
                        S.barrier()
                    if s == 0:
                        dump("onsaT", onsaT.t[:].rearrange("p a b -> p (a b)"), [onsaT.b])
                        checkpoint("nsa")
                    S.barrier()

                with contextlib.ExitStack() as mlast:
                    cqnT = alloc(mlast, "cqnT", [128, 6, S_], BF16)
                    cnT = alloc(mlast, "cnT", [128, 2, S_], BF16)
                    kropeT = alloc(mlast, "kropeT", [96, S_], BF16)
                    wuq = alloc(mlast, "wuq", [128, 6, 768], BF16)
                    wuk = alloc(mlast, "wuk", [128, 2, 512], BF16)
                    wuv = alloc(mlast, "wuv", [128, 2, 512], BF16)
                    with contextlib.ExitStack() as p1:
                        wcq = alloc(p1, "wcq", [128, 8, 768], BF16)
                        wckv = alloc(p1, "wckv", [128, 8, 288], BF16)
                        S.dma("pool", wcq.t[:], w_in_v[:, :, O_CQ:O_CQ + 768], [], [wcq.b], wcq.b)
                        S.dma("pool", wckv.t[:], w_in_v[:, :, O_CKV:O_CKV + 288], [], [wckv.b], wckv.b)
                        xT = load_xT(p1)
                        S.dma("pool", wuq.t[:], wview("mla_w_uq"), [], [wuq.b], wuq.b)
                        S.dma("pool", wuk.t[:], wview("mla_w_uk"), [], [wuk.b], wuk.b)
                        S.dma("pool", wuv.t[:], wview("mla_w_uv"), [], [wuv.b], wuv.b)
                        gq = alloc(p1, "gq", [128, 768], F32)
                        gkv = alloc(p1, "gkv", [128, 256], F32)
                        S.dma("sp", gq.t[:], dr["mla_q_norm"].partition_broadcast(128)[:, 0, :], [], [gq.b], gq.b)
                        S.dma("sp", gkv.t[:], dr["mla_kv_norm"].partition_broadcast(128)[:, 0, :], [], [gkv.b], gkv.b)
                        cos2T = cload(p1, "c_cos2T")
                        sinST = cload(p1, "c_sinST")
                        cqns = Rot([alloc(p1, "cqn%d" % i, [128, 1024], F32) for i in range(2)])
                        krs = alloc(p1, "krs", [128, 96], F32)
                        S.v("pool", "memset", [], [krs.b], ap=krs.t[:], constant=0.0)
                        junk = alloc(p1, "junk", [128, 512], BF16)
                        sst = Rot([alloc(p1, "sst%d" % i, [128, 8], F32) for i in range(2)])
                        krt = alloc(p1, "krt", [128, 64], F32)
                        rot6 = Rot(banks[0:6])
                        rot2 = Rot(banks[6:8])

                        def p1_mm(t):
                            bA, bB, bC = rot6.next(), rot6.next(), rot6.next()
                            for (bk, wt, c0, n) in ((bA, wcq, 0, 512), (bB, wcq, 512, 256), (bC, wckv, 0, 288)):
                                for kc in range(8):
                                    S.mm(bk.t[:, 0:n], xT.t[:, kc, t * 128:(t + 1) * 128], wt.t[:, kc, c0:c0 + n], kc == 0, kc == 7,
                                         [wt.b, xT.b], [bk.b], sig=(kc == 7))
                            return (bA, bB, bC)

                        def p1_rest(t, bks):
                            bA, bB, bC = bks
                            ss = sst.next()
                            S.act(junk.t[:, 0:512], bA.t[:, 0:512], AF.Square, [bA.b], [junk.b, ss.b], accum_out=ss.t[:, 0:1])
                            S.act(junk.t[:, 0:256], bB.t[:, 0:256], AF.Square, [bB.b], [junk.b, ss.b], accum_out=ss.t[:, 1:2])
                            S.act(junk.t[:, 0:256], bC.t[:, 0:256], AF.Square, [bC.b], [junk.b, ss.b], accum_out=ss.t[:, 2:3])
                            S.v("dve", "tensor_tensor", [ss.b], [ss.b], out=ss.t[:, 3:4], in0=ss.t[:, 0:1], in1=ss.t[:, 1:2], op=ALU.add)
                            S.v("dve", "tensor_scalar", [ss.b], [ss.b], out=ss.t[:, 4:5], in0=ss.t[:, 3:4], scalar1=1.0 / 768, scalar2=1e-6,
                                op0=ALU.mult, op1=ALU.add)
                            S.v("dve", "tensor_scalar", [ss.b], [ss.b], out=ss.t[:, 5:6], in0=ss.t[:, 2:3], scalar1=1.0 / 256, scalar2=1e-6,
                                op0=ALU.mult, op1=ALU.add)
                            S.act(ss.t[:, 6:8], ss.t[:, 4:6], AF.Ln, [ss.b], [ss.b])
                            S.act(ss.t[:, 4:6], ss.t[:, 6:8], AF.Exp, [ss.b], [ss.b], scale=-0.5)
                            cqn = cqns.next()
                            S.v("dve", "scalar_tensor_tensor", [bA.b, ss.b, gq.b], [cqn.b], out=cqn.t[:, 0:512], in0=bA.t[:, 0:512],
                                scalar=ss.t[:, 4:5], in1=gq.t[:, 0:512], op0=ALU.mult, op1=ALU.mult)
                            S.v("dve", "scalar_tensor_tensor", [bB.b, ss.b, gq.b], [cqn.b], out=cqn.t[:, 512:768], in0=bB.t[:, 0:256],
                                scalar=ss.t[:, 4:5], in1=gq.t[:, 512:768], op0=ALU.mult, op1=ALU.mult)
                            S.v("dve", "scalar_tensor_tensor", [bC.b, ss.b, gkv.b], [cqn.b], out=cqn.t[:, 768:1024], in0=bC.t[:, 0:256],
                                scalar=ss.t[:, 5:6], in1=gkv.t[:, :], op0=ALU.mult, op1=ALU.mult)
                            S.v("dve", "tensor_tensor", [bC.b, sinST.b], [krt.b], out=krt.t[:, 0:16], in0=bC.t[:, 272:288], in1=sinST.t[:, t, 0:16], op=ALU.mult)
                            S.v("dve", "tensor_tensor", [bC.b, sinST.b], [krt.b], out=krt.t[:, 16:32], in0=bC.t[:, 256:272], in1=sinST.t[:, t, 16:32], op=ALU.mult)
                            S.v("dve", "tensor_tensor", [bC.b, cos2T.b], [krt.b], out=krt.t[:, 32:64], in0=bC.t[:, 256:288], in1=cos2T.t[:, t, :], op=ALU.mult)
                            S.v("dve", "tensor_tensor", [krt.b], [krs.b], out=krs.t[:, 64:96], in0=krt.t[:, 0:32], in1=krt.t[:, 32:64], op=ALU.add)
                            tA, tB = rot2.next(), rot2.next()
                            for k_ in range(4):
                                S.tr(tA.t[:, k_ * 128:(k_ + 1) * 128], cqn.t[:, k_ * 128:(k_ + 1) * 128], identf.t[:], [cqn.b, identf.b], [tA.b], sig=(k_ == 3))
                            for k_ in range(4):
                                S.tr(tB.t[:, k_ * 128:(k_ + 1) * 128], cqn.t[:, (4 + k_) * 128:(5 + k_) * 128], identf.t[:], [cqn.b, identf.b], [tB.b], sig=(k_ == 3))
                            ts_ = slice(t * 128, (t + 1) * 128)
                            S.act(cqnT.t[:, 0:4, ts_], tA.t[:, :].rearrange("p (k n) -> p k n", k=4), AF.Copy, [tA.b], [cqnT.b])
                            S.v("dve", "tensor_copy", [tB.b], [cqnT.b], out=cqnT.t[:, 4:6, ts_], in_=tB.t[:, 0:256].rearrange("p (k n) -> p k n", k=2))
                            S.act(cnT.t[:, 0:2, ts_], tB.t[:, 256:512].rearrange("p (k n) -> p k n", k=2), AF.Copy, [tB.b], [cnT.b])
                            tK = rot2.next()
                            S.tr(tK.t[0:96, 0:128], krs.t[:, :], identf.t[:], [krs.b, identf.b], [tK.b])
                            S.v("dve", "tensor_copy", [tK.b], [kropeT.b], out=kropeT.t[64:96, ts_], in_=tK.t[64:96, 0:128])

                        pend = p1_mm(0)
                        for t in range(NT):
                            nxt = p1_mm(t + 1) if t + 1 < NT else None
                            p1_rest(t, pend)
                            pend = nxt
                        S.barrier()
                    if s == 0:
                        dump("cqnT", cqnT.t[:].rearrange("p a b -> p (a b)"), [cqnT.b])
                        dump("kropeT", kropeT.t[64:96, :], [kropeT.b])
                        checkpoint("mla1")
                        S.barrier()

                    with contextlib.ExitStack() as p2:
                        cosM = cload(p2, "c_cosM")
                        sinM = cload(p2, "c_sinM")
                        qmla = [alloc(p2, "qmla%d" % i, [96, S_], BF16) for i in range(4)]
                        kaug = [alloc(p2, "kaug%d" % i, [96, S_], BF16) for i in range(4)]
                        V1m = alloc(p2, "V1m", [128, NT, 4, 66], BF16)
                        S.v("pool", "memset", [], [V1m.b], ap=V1m.t[:, :, :, 64:65], constant=1.0)
                        tmps = Rot([(alloc(p2, "mqraw%d" % i, [128, 512], BF16), alloc(p2, "mrt1_%d" % i, [128, 512], F32),
                                     alloc(p2, "mrt2_%d" % i, [128, 512], F32)) for i in range(2)])
                        ptrot = Rot([alloc(p2, "mpt%d" % i, [128, 512], BF16) for i in range(3)])
                        ochunks = Rot([alloc(p2, "moch%d" % i, [128, 4, 256], F32) for i in range(2)])
                        rzs = Rot([alloc(p2, "mrz%d" % i, [128, 4], F32) for i in range(3)])
                        for hf in range(2):
                            for i in range(4):
                                h = 4 * hf + i
                                for c in range(NCH):
                                    bk = rotA.next()
                                    for kc in range(6):
                                        S.mm(bk.t[0:96, :], wuq.t[:, kc, h * 96:(h + 1) * 96], cqnT.t[:, kc, c * 512:(c + 1) * 512], kc == 0, kc == 5,
                                             [wuq.b, cqnT.b], [bk.b], sig=(kc == 5))
                                    rope_store(p2, bk, qmla[i].t[0:96, c * 512:(c + 1) * 512], 96, permM, cosM.t[0:96, c * 512:(c + 1) * 512],
                                               sinM.t[0:96, c * 512:(c + 1) * 512], [cosM.b, sinM.b], [qmla[i].b], tmps)
                                S.v("pool", "tensor_copy", [kropeT.b], [kaug[i].b], out=kaug[i].t[64:96, :], in_=kropeT.t[64:96, :])
                            for pr in range(2):
                                i0, i1 = 2 * pr, 2 * pr + 1
                                h0 = 4 * hf + i0
                                for c in range(NCH):
                                    bk = rotA.next()
                                    for kc in range(2):
                                        S.mm(bk.t[:, :], wuk.t[:, kc, h0 * 64:(h0 + 2) * 64], cnT.t[:, kc, c * 512:(c + 1) * 512], kc == 0, kc == 1,
                                             [wuk.b, cnT.b], [bk.b], sig=(kc == 1))
                                    S.v("dve", "tensor_copy", [bk.b], [kaug[i0].b], out=kaug[i0].t[0:64, c * 512:(c + 1) * 512], in_=bk.t[0:64, :])
                                    S.act(kaug[i1].t[0:64, c * 512:(c + 1) * 512], bk.t[64:128, :], AF.Copy, [bk.b], [kaug[i1].b])
                            for t in range(NT):
                                bk = rotA.next()
                                for kc in range(2):
                                    S.mm(bk.t[:, 0:256], cnT.t[:, kc, t * 128:(t + 1) * 128], wuv.t[:, kc, hf * 256:(hf + 1) * 256], kc == 0, kc == 1,
                                         [wuv.b, cnT.b], [bk.b], sig=(kc == 1))
                                if t % 2 == 0:
                                    S.act(V1m.t[:, t, :, 0:64], bk.t[:, 0:256].rearrange("p (g d) -> p g d", g=4), AF.Copy, [bk.b], [V1m.b])
                                else:
                                    S.v("dve", "tensor_copy", [bk.b], [V1m.b], out=V1m.t[:, t, :, 0:64], in_=bk.t[:, 0:256].rearrange("p (g d) -> p g d", g=4))
                            if s == 0 and hf == 0:
                                dump("qmla0", qmla[0].t[:], [qmla[0].b])
                                dump("kaug0", kaug[0].t[:], [kaug[0].b])
                            for c in range(NCH):
                                och = ochunks.next()

                                def mla_epi(acc, i, och=och):
                                    rz = rzs.next()
                                    accv = acc.t[:, 0:260].rearrange("p (b w) -> p b w", b=4)
                                    S.v("dve", "tensor_scalar", [acc.b], [rz.b], out=rz.t[:, 0:4].rearrange("p (b o) -> p b o", o=1),
                                        in0=accv[:, :, 64:65], scalar1=1e-30, scalar2=None, op0=ALU.max)
                                    S.v("dve", "reciprocal", [rz.b], [rz.b], out=rz.t[:, 0:4], in_=rz.t[:, 0:4])
                                    S.v("dve", "tensor_tensor", [acc.b, rz.b], [och.b], out=och.t[:, :, i * 64:(i + 1) * 64], in0=accv[:, :, 0:64],
                                        in1=rz.t[:, 0:4].rearrange("p (b o) -> p b o", o=1).broadcast_to([128, 4, 64]), op=ALU.mult)

                                jobs = []
                                for i in range(4):
                                    acc = rotB.next()
                                    tiles = []
                                    for t in range(4 * c + 4):
                                        if t < 4 * c:
                                            tiles.append((t, 0, 4, []))
                                        else:
                                            tiles.append((t, t - 4 * c, 4, [(0, 128, tri.t[:], tri.b)]))
                                    attn(c, tiles,
                                         lambda t, i=i: (kaug[i].t[0:96, t * 128:(t + 1) * 128], [kaug[i].b]),
                                         lambda c0, n, i=i: (qmla[i].t[0:96, c0:c0 + n], [qmla[i].b]),
                                         lambda t, i=i: (V1m.t[:, t, i, 0:65], [V1m.b]), 65, 96 ** -0.5, 128, acc, ptrot, jobs,
                                         epi=(lambda acc=acc, i=i: mla_epi(acc, i)))
                                run_jobs(jobs)
                                for j in range(2):
                                    tb = rotC.next()
                                    for b in range(4):
                                        S.tr(tb.t[:, b * 128:(b + 1) * 128], och.t[:, b, j * 128:(j + 1) * 128], identf.t[:],
                                             [och.b, identf.b], [tb.b], sig=(b == 3))
                                    if j == 0:
                                        S.act(omlaT.t[:, 2 * hf + j, c * 512:(c + 1) * 512], tb.t[:, :], AF.Copy, [tb.b], [omlaT.b])
                                    else:
                                        S.v("dve", "tensor_copy", [tb.b], [omlaT.b], out=omlaT.t[:, 2 * hf + j, c * 512:(c + 1) * 512], in_=tb.t[:, :])
                        S.barrier()
                    if s == 0:
                        dump("omlaT", omlaT.t[:].rearrange("p a b -> p (a b)"), [omlaT.b])
                        checkpoint("mla")
                    S.barrier()

                with contextlib.ExitStack() as mst:
                    xT = alloc(mst, "xT_bf", [128, 8, S_], BF16)
                    wm = Rot([alloc(mst, "wm%d" % i, [128, 24, 128], BF16) for i in range(3)])
                    gts = Rot([(alloc(mst, "g0_%d" % i, [128, 512], F32), alloc(mst, "g1_%d" % i, [128, 512], F32),
                                alloc(mst, "m0_%d" % i, [128, 512], F32), alloc(mst, "m1_%d" % i, [128, 512], F32)) for i in range(2)])
                    wa_v, wb_v = wview("nsa_w_o"), wview("mla_w_o")
                    def fetch_wm(j):
                        w = wm.next()
                        cs = slice(j * 128, (j + 1) * 128)
                        S.dma("pool", w.t[:, 0:4, :], wa_v[:, :, cs], [], [w.b], w.b)
                        S.dma("pool", w.t[:, 4:8, :], wb_v[:, :, cs], [], [w.b], w.b)
                        S.dma("pool", w.t[:, 8:16, :], w_in_v[:, :, O_MG + j * 128:O_MG + (j + 1) * 128], [], [w.b], w.b)
                        S.dma("pool", w.t[:, 16:24, :], w_in_v[:, :, O_MG + 1024 + j * 128:O_MG + 1024 + (j + 1) * 128], [], [w.b], w.b)
                        return w

                    wmq = [fetch_wm(0), fetch_wm(1)]
                    xsrc = dr["xT"][s].rearrange("(kc p) t -> p kc t", p=128)
                    for hf_ in range(2):
                        S.dma("pool", xT.t[:, :, hf_ * 1024:(hf_ + 1) * 1024], xsrc[:, :, hf_ * 1024:(hf_ + 1) * 1024], [], [xT.b], xT.b)
                    for j in range(8):
                        w = wmq.pop(0)
                        if j + 2 < 8:
                            wmq.append(fetch_wm(j + 2))
                        for c in range(NCH):
                            tc_ = slice(c * 512, (c + 1) * 512)
                            ya, yb, m0, m1 = rotA.next(), rotA.next(), rotB.next(), rotB.next()
                            for kc in range(4):
                                S.mm(ya.t[:, :], w.t[:, kc, :], onsaT.t[:, kc, tc_], kc == 0, kc == 3, [w.b, onsaT.b], [ya.b], sig=(kc == 3))
                            for kc in range(4):
                                S.mm(yb.t[:, :], w.t[:, 4 + kc, :], omlaT.t[:, kc, tc_], kc == 0, kc == 3, [w.b, omlaT.b], [yb.b], sig=(kc == 3))
                            for kc in range(8):
                                S.mm(m0.t[:, :], w.t[:, 8 + kc, :], xT.t[:, kc, tc_], kc == 0, kc == 7, [w.b, xT.b], [m0.b], sig=(kc == 7))
                            for kc in range(8):
                                S.mm(m1.t[:, :], w.t[:, 16 + kc, :], xT.t[:, kc, tc_], kc == 0, kc == 7, [w.b, xT.b], [m1.b], sig=(kc == 7))
                            g0, g1, t0, t1 = gts.next()
                            S.act(g0.t[:], m0.t[:, :], AF.Sigmoid, [m0.b], [g0.b])
                            S.act(g1.t[:], m1.t[:, :], AF.Sigmoid, [m1.b], [g1.b])
                            S.v("dve", "tensor_tensor", [ya.b, g0.b], [t0.b], out=t0.t[:], in0=ya.t[:, :], in1=g0.t[:], op=ALU.mult)
                            S.v("dve", "tensor_tensor", [yb.b, g1.b], [t1.b], out=t1.t[:], in0=yb.t[:, :], in1=g1.t[:], op=ALU.mult)
                            S.v("dve", "tensor_tensor", [t0.b, t1.b], [mergedT.b], out=mergedT.t[:, j, tc_], in0=t0.t[:], in1=t1.t[:], op=ALU.add)
                    S.barrier()
                S.barrier()
            if s == 0:
                dump("mergedT", mergedT.t[:].rearrange("p a b -> p (a b)"), [mergedT.b])
                checkpoint("merge")
                S.barrier()

            with contextlib.ExitStack() as fst:
                wout = alloc(fst, "wout", [128, 8, DM], BF16)
                wdn = alloc(fst, "wdn", [128, NJ, DM], BF16)
                for hh in range(2):
                    S.dma("pool", wout.t[:, :, hh * 512:(hh + 1) * 512], wview("w_out")[:, :, hh * 512:(hh + 1) * 512], [], [wout.b], wout.b)
                lnp = {}
                for nm in ("ln1_g", "ln1_b", "ln2_g", "ln2_b"):
                    lnp[nm] = alloc(fst, nm, [128, DM], F32)
                    S.dma("sp", lnp[nm].t[:], dr[nm].partition_broadcast(128)[:, 0, :], [], [lnp[nm].b], lnp[nm].b)
                cw = alloc(fst, "cw", [128, 3, NJ], F32)
                cb = alloc(fst, "cb", [128, NJ], F32)
                S.dma("sp", cw.t[:], dr["ffn_conv_w"], [], [cw.b], cw.b)
                S.dma("sp", cb.t[:], dr["ffn_conv_b"], [], [cb.b], cb.b)
                halo = alloc(fst, "halo", [128, NJ, 2], F32)
                S.v("pool", "memset", [], [halo.b], ap=halo.t[:], constant=0.0)
                x1 = alloc(fst, "x1", [128, 4, DM], F32)
                x1b = [Buf("x1_%d" % i) for i in range(4)]
                x1T = alloc(fst, "x1T", [128, 8, 512], BF16)
                hT = alloc(fst, "hT", [128, NJ, 512], BF16)
                xres = Rot([alloc(fst, "xres%d" % i, [128, DM], F32) for i in range(1)])
                rbuf = Rot([alloc(fst, "rbuf%d" % i, [128, DM], F32) for i in range(3)])
                lntmp = (alloc(fst, "lnstats", [128, 2, 6], F32), alloc(fst, "lnmv", [128, 2], F32), alloc(fst, "lnsc", [128, 4], F32))
                lntmp2 = (alloc(fst, "lnstats2", [128, 2, 6], F32), alloc(fst, "lnmv2", [128, 2], F32), alloc(fst, "lnsc2", [128, 4], F32))
                otile = Rot([alloc(fst, "otile%d" % i, [128, DM], F32) for i in range(1)])
                wgu = Rot([alloc(fst, "wgu%d" % i, [128, 2, 8, 128], BF16) for i in range(4)])
                a_sb = Rot([alloc(fst, "a_sb%d" % i, [128, 514], F32) for i in range(2)])
                ct = Rot([(alloc(fst, "ct0_%d" % i, [128, 512], F32), alloc(fst, "ct1_%d" % i, [128, 512], F32)) for i in range(2)])
                wg_v, wu_v = wview("ffn_w_gate"), wview("ffn_w_up")
                PF = 3
                wq = []
                rotU = Rot(banks[3:6])

                def fetch_w(j):
                    w = wgu.next()
                    cs = slice(j * 128, (j + 1) * 128)
                    S.dma("pool", w.t[:, 0, :, :], wg_v[:, :, cs], [], [w.b], w.b)
                    S.dma("pool", w.t[:, 1, :, :], wu_v[:, :, cs], [], [w.b], w.b)
                    return w

                def stage1_mm(blk, tt):
                    t = blk * 4 + tt
                    xr = xres.next()
                    S.dma("sp", xr.t[:], dr["x"][s, t * 128:(t + 1) * 128, :], [], [xr.b], xr.b)
                    bks = []
                    for hh in range(2):
                        bk = rotU.next()
                        for kc in range(8):
                            S.mm(bk.t[:, :], mergedT.t[:, kc, t * 128:(t + 1) * 128], wout.t[:, kc, hh * 512:(hh + 1) * 512], kc == 0, kc == 7,
                                 [mergedT.b, wout.b], [bk.b], sig=(kc == 7))
                        bks.append(bk)
                    return xr, bks

                def stage1_res(xr, bks):
                    r = rbuf.next()
                    for hh in range(2):
                        bk = bks[hh]
                        S.v("dve", "scalar_tensor_tensor", [xr.b, bk.b], [r.b], out=r.t[:, hh * 512:(hh + 1) * 512], in0=xr.t[:, hh * 512:(hh + 1) * 512],
                            scalar=ALPHA, in1=bk.t[:, :], op0=ALU.mult, op1=ALU.add)
                    return r

                def stage1_tr(tt):
                    for hh in range(2):
                        tb = rotC.next()
                        for k_ in range(4):
                            S.tr(tb.t[:, k_ * 128:(k_ + 1) * 128], x1.t[:, tt, (hh * 4 + k_) * 128:(hh * 4 + k_ + 1) * 128], identf.t[:],
                                 [x1b[tt], identf.b], [tb.b], sig=(k_ == 3))
                        if hh == 0:
                            S.act(x1T.t[:, 0:4, tt * 128:(tt + 1) * 128], tb.t[:, :].rearrange("p (k n) -> p k n", k=4), AF.Copy, [tb.b], [x1T.b])
                        else:
                            S.v("dve", "tensor_copy", [tb.b], [x1T.b], out=x1T.t[:, 4:8, tt * 128:(tt + 1) * 128],
                                in_=tb.t[:, :].rearrange("p (k n) -> p k n", k=4))

                def stage2(blk):
                    for j in range(NJ):
                        nxt = blk * NJ + j + PF
                        if nxt < 4 * NJ:
                            wq.append(fetch_w(nxt % NJ))
                        w = wq.pop(0)
                        ba, bu = rotA.next(), rotU.next()
                        for kc in range(8):
                            S.mm(ba.t[:, :], w.t[:, 0, kc, :], x1T.t[:, kc, :], kc == 0, kc == 7, [w.b, x1T.b], [ba.b], sig=(kc == 7))
                        for kc in range(8):
                            S.mm(bu.t[:, :], w.t[:, 1, kc, :], x1T.t[:, kc, :], kc == 0, kc == 7, [w.b, x1T.b], [bu.b], sig=(kc == 7))
                        asb = a_sb.next()
                        S.act(asb.t[:, 0:2], halo.t[:, j, :], AF.Copy, [halo.b], [asb.b])
                        S.act(asb.t[:, 2:514], ba.t[:, :], AF.Copy, [ba.b], [asb.b])
                        S.act(halo.t[:, j, :], asb.t[:, 512:514], AF.Copy, [asb.b], [halo.b])
                        c0, c1 = ct.next()
                        S.act(c0.t[:], ba.t[:, :], AF.Identity, [ba.b, cw.b, cb.b], [c0.b], scale=cw.t[:, 2, j:j + 1], bias=cb.t[:, j:j + 1])
                        S.v("dve", "scalar_tensor_tensor", [asb.b, cw.b, c0.b], [c1.b], out=c1.t[:], in0=asb.t[:, 1:513], scalar=cw.t[:, 1, j:j + 1],
                            in1=c0.t[:], op0=ALU.mult, op1=ALU.add)
                        S.v("dve", "scalar_tensor_tensor", [asb.b, cw.b, c1.b], [c0.b], out=c0.t[:], in0=asb.t[:, 0:512], scalar=cw.t[:, 0, j:j + 1],
                            in1=c1.t[:], op0=ALU.mult, op1=ALU.add)
                        S.act(c1.t[:], c0.t[:], AF.Gelu_apprx_tanh, [c0.b], [c1.b])
                        S.v("dve", "tensor_tensor", [c1.b, bu.b], [hT.b], out=hT.t[:, j, :], in0=bu.t[:, :], in1=c1.t[:], op=ALU.mult)

                def stage3_mm(blk, tt):
                    bks = []
                    for hh in range(2):
                        bk = rotA.next()
                        for j in range(NJ):
                            S.mm(bk.t[:, :], hT.t[:, j, tt * 128:(tt + 1) * 128], wdn.t[:, j, hh * 512:(hh + 1) * 512], j == 0, j == NJ - 1,
                                 [hT.b, wdn.b], [bk.b], sig=(j == NJ - 1))
                        bks.append(bk)
                    return bks

                def stage3_res(tt, bks):
                    r = rbuf.next()
                    for hh in range(2):
                        bk = bks[hh]
                        S.v("dve", "scalar_tensor_tensor", [x1b[tt], bk.b], [r.b], out=r.t[:, hh * 512:(hh + 1) * 512], in0=x1.t[:, tt, hh * 512:(hh + 1) * 512],
                            scalar=ALPHA, in1=bk.t[:, :], op0=ALU.mult, op1=ALU.add)
                    return r

                def stage3_ln(blk, tt, r):
                    t = blk * 4 + tt
                    ot = otile.next()
                    layer_norm(r, lnp["ln2_g"], lnp["ln2_b"], ot.t[:], ot.b, lntmp2)
                    S.dma("sp", out_d[s, t * 128:(t + 1) * 128, :], ot.t[:], [ot.b], [], outq)

                for jj in range(PF):
                    wq.append(fetch_w(jj))
                for j0 in range(0, NJ, 6):
                    j1 = min(NJ, j0 + 6)
                    S.dma("pool", wdn.t[:, j0:j1, :], wview("ffn_w_down")[:, j0:j1, :], [], [wdn.b], wdn.b)
                for blk in range(5):
                    for tt in range(4):
                        if blk < 4:
                            xr_, bks_ = stage1_mm(blk, tt)
                        bks3 = stage3_mm(blk - 1, tt) if blk >= 1 else None
                        if blk < 4:
                            r1 = stage1_res(xr_, bks_)
                            layer_norm(r1, lnp["ln1_g"], lnp["ln1_b"], None, None, lntmp)
                        if blk >= 1:
                            r2 = stage3_res(tt, bks3)
                        if blk < 4:
                            ln_bias(r1, lnp["ln1_b"], x1.t[:, tt, :], x1b[tt])
                        if blk >= 1:
                            stage3_ln(blk - 1, tt, r2)
                        if blk < 4 and tt >= 1:
                            stage1_tr(tt - 1)
                    if blk < 4:
                        stage1_tr(3)
                    if s == 0 and blk == 0:
                        dump("x1", x1.t[:].rearrange("p a b -> p (a b)"), x1b)
                    if blk < 4:
                        stage2(blk)
                S.barrier()
            S.barrier()
        S.barrier()
    return nc


def _host_inputs(inputs):
    f = lambda a: np.ascontiguousarray(np.asarray(a, dtype=np.float32))
    shared = {}
    for k_, shp in _W_SHAPES.items():
        if k_ == "pe_kT":
            a = f(inputs["cmp_pe_k"])[0].T
        elif k_ == "pe_vT":
            a = f(inputs["cmp_pe_v"])[0].T
        elif k_ in ("cmp_k_b1", "cmp_v_b1"):
            a = f(inputs[k_])[0].reshape(2, 128).T
        elif k_ == "ffn_conv_w":
            a = f(inputs[k_])[0].reshape(3, NJ, 128).transpose(2, 0, 1)
        elif k_ == "ffn_conv_b":
            a = f(inputs[k_])[0].reshape(NJ, 128).T
        else:
            a = f(inputs[k_])[0]
        shared[k_] = np.ascontiguousarray(a.reshape(shp))
    shared.update(_consts())
    return shared


def kernel(**inputs):
    x = np.ascontiguousarray(np.asarray(inputs["x"], dtype=np.float32))
    shared = _host_inputs(inputs)
    nc = build(NSEQ)
    in_maps = []
    for c in range(NCORES):
        xs = x[c * NSEQ:(c + 1) * NSEQ]
        m = dict(shared)
        m["x"] = np.ascontiguousarray(xs)
        m["xT"] = np.ascontiguousarray(xs.transpose(0, 2, 1))
        in_maps.append(m)
    res = run_bass_kernel_spmd(nc, in_maps, core_ids=list(range(NCORES)))
    return np.concatenate([np.asarray(r["out"], dtype=np.float32) for r in res.results], axis=0)
```
